# Optimizing a Trainium2 kernel written in Bass

```python
import math
import jax, jax.numpy as jnp
from jax import lax
import numpy as np

D_MODEL = 2048
BATCH = 2
SEQ = 8192
DEPTH = 2

CHUNK = 64
QBLOCK = 128
N_MIXERS = 2
FOX_HEADS = 16
FOX_HEAD_DIM = D_MODEL // FOX_HEADS
RET_HEADS = 8
RET_QK_DIM = D_MODEL // RET_HEADS
RET_QK_WIDTH = RET_HEADS * RET_QK_DIM
RET_V_WIDTH = 2 * D_MODEL
RET_V_DIM = RET_V_WIDTH // RET_HEADS
ROPE_BASE = 10000.0
PEER_HEADS = 8
PEER_N_KEYS = 128
PEER_N_EXPERTS = PEER_N_KEYS * PEER_N_KEYS
PEER_QUERY_DIM = 256
PEER_HALF = PEER_QUERY_DIM // 2
PEER_TOPK = 16
PEER_TOKEN_BLOCK = 128
DN_ALPHA = (2 * DEPTH) ** 0.25
DN_BETA = (8 * DEPTH) ** -0.25
LN_EPS = 1e-5

kernel_name = "fox_retnet_peer_deepnorm_hybrid"


def layer_norm(x, g, b):
    xf = x.astype(jnp.float32)
    mu = jnp.mean(xf, axis=-1, keepdims=True)
    var = jnp.mean(jnp.square(xf - mu), axis=-1, keepdims=True)
    y = (xf - mu) * lax.rsqrt(var + LN_EPS)
    return (y * g + b).astype(x.dtype)


def rope(a, pos):
    half = a.shape[-1] // 2
    inv_freq = ROPE_BASE ** (-jnp.arange(half, dtype=jnp.float32) / half)
    ang = pos[:, None] * inv_freq[None, :]
    cos = jnp.cos(ang)[None, :, None, :]
    sin = jnp.sin(ang)[None, :, None, :]
    a1, a2 = a[..., :half], a[..., half:]
    return jnp.concatenate([a1 * cos - a2 * sin, a1 * sin + a2 * cos], axis=-1)


def fox_mixer(x, w_in, b_f, w_o):
    B, S, D = x.shape
    proj = x @ w_in
    q = proj[..., :D].reshape(B, S, FOX_HEADS, FOX_HEAD_DIM) * (FOX_HEAD_DIM ** -0.5)
    k = proj[..., D:2 * D].reshape(B, S, FOX_HEADS, FOX_HEAD_DIM)
    v = proj[..., 2 * D:3 * D].reshape(B, S, FOX_HEADS, FOX_HEAD_DIM)
    log_f = jax.nn.log_sigmoid((proj[..., 3 * D:] + b_f).astype(jnp.float32))
    c = jnp.cumsum(log_f, axis=1).transpose(0, 2, 1)
    outs = []
    for blk in range(S // QBLOCK):
        lo, hi = blk * QBLOCK, (blk + 1) * QBLOCK
        logits = jnp.einsum('bqhd,bkhd->bhqk', q[:, lo:hi], k[:, :hi]).astype(jnp.float32)
        logits = logits + c[:, :, lo:hi, None] - c[:, :, None, :hi]
        mask = jnp.arange(lo, hi)[:, None] >= jnp.arange(hi)[None, :]
        logits = jnp.where(mask, logits, -jnp.inf)
        p = jax.nn.softmax(logits, axis=-1).astype(v.dtype)
        outs.append(jnp.einsum('bhqk,bkhd->bqhd', p, v[:, :hi]))
    o = jnp.concatenate(outs, axis=1).reshape(B, S, D)
    return o @ w_o


def retention_mixer(x, w_in, gn_g, w_o):
    B, S, D = x.shape
    C = CHUNK
    NC = S // C
    proj = x @ w_in
    o1, o2, o3 = RET_QK_WIDTH, 2 * RET_QK_WIDTH, 2 * RET_QK_WIDTH + RET_V_WIDTH
    q = proj[..., :o1].reshape(B, S, RET_HEADS, RET_QK_DIM).astype(jnp.float32)
    k = proj[..., o1:o2].reshape(B, S, RET_HEADS, RET_QK_DIM).astype(jnp.float32) * (RET_QK_DIM ** -0.5)
    v = proj[..., o2:o3].reshape(B, S, RET_HEADS, RET_V_DIM).astype(jnp.float32)
    gate = proj[..., o3:].astype(jnp.float32)
    pos = jnp.arange(S, dtype=jnp.float32)
    q = rope(q, pos)
    k = rope(k, pos)

    log_gamma = jnp.log(1.0 - jnp.exp2(-5.0 - jnp.arange(RET_HEADS, dtype=jnp.float32)))
    t = jnp.arange(C, dtype=jnp.float32)
    intra_decay = jnp.exp(log_gamma[:, None, None] * jnp.abs(t[:, None] - t[None, :]))
    q_decay = jnp.exp(log_gamma[:, None] * (t[None, :] + 1.0))
    k_decay = jnp.exp(log_gamma[:, None] * (C - 1.0 - t[None, :]))
    chunk_decay = jnp.exp(log_gamma * C)

    def to_chunks(a):
        return a.reshape(B, NC, C, RET_HEADS, a.shape[-1]).transpose(1, 0, 3, 2, 4)

    def step(state, inp):
        qc, kc, vc = inp
        scores = jnp.einsum('bhtd,bhsd->bhts', qc, kc) * intra_decay
        out = jnp.einsum('bhts,bhse->bhte', scores, vc)
        out = out + jnp.einsum('bhtd,bhde->bhte', qc * q_decay[None, :, :, None], state)
        state = state * chunk_decay[None, :, None, None] + jnp.einsum(
            'bhsd,bhse->bhde', kc * k_decay[None, :, :, None], vc)
        return state, out

    state0 = jnp.zeros((B, RET_HEADS, RET_QK_DIM, RET_V_DIM), jnp.float32)
    _, o = lax.scan(step, state0, (to_chunks(q), to_chunks(k), to_chunks(v)))
    o = o.transpose(1, 0, 3, 2, 4).reshape(B, S, RET_HEADS, RET_V_DIM)
    mu = jnp.mean(o, axis=-1, keepdims=True)
    var = jnp.mean(jnp.square(o - mu), axis=-1, keepdims=True)
    o = ((o - mu) * lax.rsqrt(var + LN_EPS)).reshape(B, S, RET_V_WIDTH) * gn_g
    return (jax.nn.silu(gate) * o).astype(x.dtype) @ w_o


def peer_ffn(x, wq, sub_k1, sub_k2, u, v):
    B, S, D = x.shape
    T = B * S
    K = PEER_TOPK
    xt = x.reshape(T, D)
    q = (xt @ wq).reshape(T, PEER_HEADS, PEER_QUERY_DIM)
    s1 = jnp.einsum('thd,nd->thn', q[..., :PEER_HALF], sub_k1).astype(jnp.float32)
    s2 = jnp.einsum('thd,nd->thn', q[..., PEER_HALF:], sub_k2).astype(jnp.float32)
    v1, i1 = lax.top_k(s1, K)
    v2, i2 = lax.top_k(s2, K)
    cand = (v1[..., :, None] + v2[..., None, :]).reshape(T, PEER_HEADS, K * K)
    sc, ci = lax.top_k(cand, K)
    e1 = jnp.take_along_axis(i1, ci // K, axis=-1)
    e2 = jnp.take_along_axis(i2, ci % K, axis=-1)
    eid = e1 * PEER_N_KEYS + e2
    g = jax.nn.softmax(sc, axis=-1)

    nb = T // PEER_TOKEN_BLOCK

    def block(args):
        xb, eb, gb = args
        h = jnp.einsum('td,thkd->thk', xb, u[eb]).astype(jnp.float32)
        w = (gb * jax.nn.gelu(h, approximate=False)).astype(xb.dtype)
        return jnp.einsum('thk,thkd->td', w, v[eb])

    out = lax.map(block, (xt.reshape(nb, PEER_TOKEN_BLOCK, D),
                          eid.reshape(nb, PEER_TOKEN_BLOCK, PEER_HEADS, K),
                          g.reshape(nb, PEER_TOKEN_BLOCK, PEER_HEADS, K)))
    return out.reshape(B, S, D)


def _normal(key, shape, std):
    return jax.random.normal(key, shape, jnp.float32) * std


def _ln_params(key):
    k1, k2 = jax.random.split(key)
    return 1.0 + _normal(k1, (D_MODEL,), 0.02), _normal(k2, (D_MODEL,), 0.02)


def _peer_params(key):
    ks = jax.random.split(key, 5)
    wq = _normal(ks[0], (D_MODEL, PEER_HEADS * PEER_QUERY_DIM), D_MODEL ** -0.5)
    k1 = _normal(ks[1], (PEER_N_KEYS, PEER_HALF), PEER_HALF ** -0.5)
    k2 = _normal(ks[2], (PEER_N_KEYS, PEER_HALF), PEER_HALF ** -0.5)
    u = _normal(ks[3], (PEER_N_EXPERTS, D_MODEL), D_MODEL ** -0.5)
    v = _normal(ks[4], (PEER_N_EXPERTS, D_MODEL), DN_BETA * PEER_HEADS ** -0.5)
    return wq, k1, k2, u, v


def setup_inputs(seed: int = 0) -> dict:
    key = jax.random.key(seed)
    ks = jax.random.split(key, 16)
    D = D_MODEL
    x = jax.random.normal(ks[0], (BATCH, SEQ, D), jnp.float32)

    fox_cols = 3 * D + FOX_HEADS
    fox_scale = jnp.concatenate([jnp.ones((2 * D,), jnp.float32),
                                 jnp.full((D,), DN_BETA, jnp.float32),
                                 jnp.ones((FOX_HEADS,), jnp.float32)])
    l0_fox_w_in = _normal(ks[1], (D, fox_cols), D ** -0.5) * fox_scale
    l0_fox_b_f = jnp.linspace(3.0, 6.0, FOX_HEADS, dtype=jnp.float32) + _normal(ks[2], (FOX_HEADS,), 0.1)
    l0_fox_w_o = _normal(ks[3], (D, D), DN_BETA * D ** -0.5)
    l0_ln1_g, l0_ln1_b = _ln_params(ks[4])
    l0_peer_wq, l0_peer_k1, l0_peer_k2, l0_peer_u, l0_peer_v = _peer_params(ks[5])
    l0_ln2_g, l0_ln2_b = _ln_params(ks[6])

    ret_cols = 2 * RET_QK_WIDTH + 2 * RET_V_WIDTH
    ret_scale = jnp.concatenate([jnp.ones((2 * RET_QK_WIDTH,), jnp.float32),
                                 jnp.full((RET_V_WIDTH,), DN_BETA, jnp.float32),
                                 jnp.ones((RET_V_WIDTH,), jnp.float32)])
    l1_ret_w_in = _normal(ks[7], (D, ret_cols), D ** -0.5) * ret_scale
    l1_ret_gn_g = 1.0 + _normal(ks[8], (RET_V_WIDTH,), 0.02)
    l1_ret_w_o = _normal(ks[9], (RET_V_WIDTH, D), DN_BETA * RET_V_WIDTH ** -0.5)
    l1_ln1_g, l1_ln1_b = _ln_params(ks[10])
    l1_peer_wq, l1_peer_k1, l1_peer_k2, l1_peer_u, l1_peer_v = _peer_params(ks[11])
    l1_ln2_g, l1_ln2_b = _ln_params(ks[12])

    return {
        "x": x,
        "l0_fox_w_in": l0_fox_w_in, "l0_fox_b_f": l0_fox_b_f, "l0_fox_w_o": l0_fox_w_o,
        "l0_ln1_g": l0_ln1_g, "l0_ln1_b": l0_ln1_b,
        "l0_peer_wq": l0_peer_wq, "l0_peer_k1": l0_peer_k1, "l0_peer_k2": l0_peer_k2,
        "l0_peer_u": l0_peer_u, "l0_peer_v": l0_peer_v,
        "l0_ln2_g": l0_ln2_g, "l0_ln2_b": l0_ln2_b,
        "l1_ret_w_in": l1_ret_w_in, "l1_ret_gn_g": l1_ret_gn_g, "l1_ret_w_o": l1_ret_w_o,
        "l1_ln1_g": l1_ln1_g, "l1_ln1_b": l1_ln1_b,
        "l1_peer_wq": l1_peer_wq, "l1_peer_k1": l1_peer_k1, "l1_peer_k2": l1_peer_k2,
        "l1_peer_u": l1_peer_u, "l1_peer_v": l1_peer_v,
        "l1_ln2_g": l1_ln2_g, "l1_ln2_b": l1_ln2_b,
    }


def reference(x,
              l0_fox_w_in, l0_fox_b_f, l0_fox_w_o, l0_ln1_g, l0_ln1_b,
              l0_peer_wq, l0_peer_k1, l0_peer_k2, l0_peer_u, l0_peer_v, l0_ln2_g, l0_ln2_b,
              l1_ret_w_in, l1_ret_gn_g, l1_ret_w_o, l1_ln1_g, l1_ln1_b,
              l1_peer_wq, l1_peer_k1, l1_peer_k2, l1_peer_u, l1_peer_v, l1_ln2_g, l1_ln2_b):
    layers = [
        ((l0_fox_w_in, l0_fox_b_f, l0_fox_w_o), (l0_ln1_g, l0_ln1_b),
         (l0_peer_wq, l0_peer_k1, l0_peer_k2, l0_peer_u, l0_peer_v), (l0_ln2_g, l0_ln2_b)),
        ((l1_ret_w_in, l1_ret_gn_g, l1_ret_w_o), (l1_ln1_g, l1_ln1_b),
         (l1_peer_wq, l1_peer_k1, l1_peer_k2, l1_peer_u, l1_peer_v), (l1_ln2_g, l1_ln2_b)),
    ]
    for i in range(DEPTH):
        mixer_p, ln1_p, peer_p, ln2_p = layers[i]
        if i % N_MIXERS == 0:
            mixed = fox_mixer(x, *mixer_p)
        else:
            mixed = retention_mixer(x, *mixer_p)
        x = layer_norm(DN_ALPHA * x + mixed, *ln1_p)
        x = layer_norm(DN_ALPHA * x + peer_ffn(x, *peer_p), *ln2_p)
    return x
```

```python
import contextlib
import numpy as np
import concourse.bass as bass
import concourse.mybir as mybir
from concourse.bass_utils import run_bass_kernel_spmd

F32 = mybir.dt.float32
BF16 = mybir.dt.bfloat16
AF = mybir.ActivationFunctionType
ALU = mybir.AluOpType
AX = mybir.AxisListType

ENGS = ("pe", "act", "dve", "pool", "sp")
N_DMA_SEMS = 40


class _Ins:
    __slots__ = ("eng", "fn", "deps", "inc", "dma", "idx")

    def __init__(self, eng, fn, dma=None):
        self.eng = eng
        self.fn = fn
        self.deps = []
        self.inc = False
        self.dma = dma
        self.idx = None


class Prog:
    def __init__(self, nc):
        self.nc = nc
        self.streams = {e: [] for e in ENGS}
        self.last_w = {}
        self.readers = {}
        self.known = {e: {} for e in ENGS}
        self.known_dma = {e: set() for e in ENGS}
        self.dmas = []
        self.sem_uses = [0] * N_DMA_SEMS
        self.sem_last = [None] * N_DMA_SEMS

    def _need(self, ins, tok):
        if tok is None:
            return
        if tok[0] == "e":
            _, se, si = tok
            if se == ins.eng and ins.dma is None:
                return
            if self.known[ins.eng].get(se, -1) >= si:
                return
            self.known[ins.eng][se] = si
            ins.deps.append(tok)
        else:
            if tok[1] in self.known_dma[ins.eng]:
                return
            self.known_dma[ins.eng].add(tok[1])
            ins.deps.append(tok)

    def _track(self, ins, reads, writes):
        eng = ins.eng
        me = ("d", ins.dma) if ins.dma is not None else ("e", eng, ins.idx)
        for r in reads:
            w = self.last_w.get(r)
            if w is not None:
                if w[0] == "e" and w[1] == eng and ins.dma is None:
                    if eng != "pe" and self.known[eng].get(eng, -1) < w[2]:
                        self.known[eng][eng] = w[2]
                        ins.deps.append(w)
                else:
                    self._need(ins, w)
            self.readers.setdefault(r, []).append(me)
        for wr in writes:
            w = self.last_w.get(wr)
            if w is not None and not (w[0] == "e" and w[1] == eng and ins.dma is None):
                self._need(ins, w)
            for rd in self.readers.get(wr, ()):
                if rd[0] == "e" and rd[1] == eng and ins.dma is None:
                    continue
                self._need(ins, rd)
            self.readers[wr] = []
            self.last_w[wr] = me

    def op(self, eng, fn, reads=(), writes=()):
        ins = _Ins(eng, fn)
        ins.idx = len(self.streams[eng])
        self._track(ins, reads, writes)
        self.streams[eng].append(ins)
        return ins

    def dma(self, out, in_, reads=(), writes=(), queue="sp", **kw):
        did = len(self.dmas)
        slot = did % N_DMA_SEMS
        self.sem_uses[slot] += 1
        cnt = self.sem_uses[slot]
        prev = self.sem_last[slot]
        self.sem_last[slot] = did
        self.dmas.append((slot, cnt))
        ins = _Ins(queue, lambda e: e.dma_start(out=out, in_=in_, **kw), dma=did)
        ins.idx = len(self.streams[queue])
        if prev is not None:
            self._need(ins, ("d", prev))
        self._track(ins, reads, writes)
        self.streams[queue].append(ins)
        return ins

    def emit(self, final_wait_all=True):
        nc = self.nc
        for e in ENGS:
            for ins in self.streams[e]:
                for d in ins.deps:
                    if d[0] == "e":
                        self.streams[d[1]][d[2]].inc = True
        vals = {}
        for e in ENGS:
            c = 0
            for ins in self.streams[e]:
                if ins.dma is None and ins.inc:
                    c += 1
                    vals[(e, ins.idx)] = c
        with contextlib.ExitStack() as st:
            esem = {e: st.enter_context(nc.semaphore("s_" + e)) for e in ENGS}
            dsem = [st.enter_context(nc.semaphore("d_%d" % i)) for i in range(N_DMA_SEMS)]
            block = st.enter_context(nc.Block())
            dmas = self.dmas
            sem_uses = self.sem_uses

            def run(e, engine):
                for ins in self.streams[e]:
                    for d in ins.deps:
                        if d[0] == "e":
                            engine.wait_ge(esem[d[1]], vals[(d[1], d[2])])
                        else:
                            slot, cnt = dmas[d[1]]
                            engine.wait_ge(dsem[slot], 16 * cnt)
                    r = ins.fn(engine)
                    if ins.dma is not None:
                        slot, cnt = dmas[ins.dma]
                        r.then_inc(dsem[slot], 16)
                    elif ins.inc:
                        r.then_inc(esem[e], 1)
                if e == "sp" and final_wait_all:
                    for slot in range(N_DMA_SEMS):
                        if sem_uses[slot]:
                            engine.wait_ge(dsem[slot], 16 * sem_uses[slot])

            @block.sync
            def _(eng):
                run("sp", eng)

            @block.tensor
            def _(eng):
                run("pe", eng)

            @block.scalar
            def _(eng):
                run("act", eng)

            @block.vector
            def _(eng):
                run("dve", eng)

            @block.gpsimd
            def _(eng):
                run("pool", eng)


DN_ALPHA = 4 ** 0.25
LN_EPS = 1e-5
NEG = -1.0e30


def build_post(KD, NT):
    nc = bass.Bass("TRN2", target_bir_lowering=False)
    KC = KD // 128
    T = NT * 128
    mT = nc.dram_tensor("mT", [KC, 128, T], BF16, kind="ExternalInput").ap()
    wo = nc.dram_tensor("wo", [KC, 128, 2048], BF16, kind="ExternalInput").ap()
    xres = nc.dram_tensor("xres", [T, 2048], F32, kind="ExternalInput").ap()
    lnp_d = nc.dram_tensor("lnp", [4, 2048], F32, kind="ExternalInput").ap()
    wq = nc.dram_tensor("wq", [16, 128, 2048], BF16, kind="ExternalInput").ap()
    k12_d = nc.dram_tensor("k12T", [128, 2, 128], F32, kind="ExternalInput").ap()
    idf_d = nc.dram_tensor("identf", [128, 128], F32, kind="ExternalInput").ap()
    idb_d = nc.dram_tensor("identb", [128, 128], BF16, kind="ExternalInput").ap()
    uT = nc.dram_tensor("uT", [32, 128, 8192], BF16, kind="ExternalInput").ap()
    vv = nc.dram_tensor("vv", [32, 128, 8192], BF16, kind="ExternalInput").ap()
    out = nc.dram_tensor("out", [T, 2048], F32, kind="ExternalOutput").ap()
    outb = nc.dram_tensor("outb", [T, 2048], BF16, kind="ExternalOutput").ap()

    with contextlib.ExitStack() as st:
        def sb(name, shape, dt):
            return st.enter_context(nc.sbuf_tensor(name, shape, dt))

        def ps(name, shape, dt):
            return st.enter_context(nc.psum_tensor(name, shape, dt))

        lnb = sb("lnb", [128, 2048], F32)
        idf = sb("idf", [128, 128], F32)
        idb = sb("idb", [128, 128], BF16)
        k12f = sb("k12f", [128, 2, 128], F32)
        k12b = sb("k12b", [128, 2, 128], BF16)
        mt = [sb("mt%d" % i, [128, KC, 128], BF16) for i in range(1)]
        xr = sb("xr", [128, 2048], F32)
        qsb = xr
        x1 = sb("x1", [128, 2048], F32)
        x1T = sb("x1T", [128, 16, 128], BF16)
        qT = sb("qT", [128, 16, 128], BF16)
        s1 = sb("s1", [128, 8, 128], F32)
        s2 = sb("s2", [128, 8, 128], F32)
        m1 = sb("m1", [128, 8, 16], F32)
        m2 = sb("m2", [128, 8, 16], F32)
        tmp = sb("tmp", [128, 256], F32)
        cand = sb("cand", [128, 16, 16], F32)
        sc16 = sb("sc16", [128, 8, 16], F32)
        ex = sb("ex", [128, 8, 16], F32)
        zz = sb("zz", [128, 8], F32)
        nlnz = sb("nlnz", [128, 8], F32)
        st6 = sb("st6", [128, 4, 6], F32)
        mv = sb("mv", [128, 2], F32)
        rstd = sb("rstd", [128, 1], F32)
        G = sb("G", [128, 128, 128], BF16)
        GE = 16
        Db = [sb("D%d" % i, [128, GE, 128], F32) for i in range(2)]
        Eb = [sb("E%d" % i, [128, GE, 128], BF16) for i in range(2)]
        Gh = [sb("Gh%d" % i, [128, GE, 128], BF16) for i in range(2)]
        NUB = 2
        NCB = 4
        NVB = 2
        ub = [sb("ub%d" % i, [128, 16, 512], BF16) for i in range(NUB)]
        vb = [sb("vb%d" % i, [128, 4, 2048], BF16) for i in range(NVB)]
        cb = [sb("cb%d" % i, [128, 2048], BF16) for i in range(NCB)]
        gel = [sb("gel%d" % i, [128, 512], BF16) for i in range(2)]
        wsb = [sb("wsb%d" % i, [128, 512], BF16) for i in range(2)]
        wt = [sb("wt%d" % i, [128, 4, 128], BF16) for i in range(2)]

        acc4 = ps("acc4", [128, 2048], F32)
        pab = [ps("pa", [128, 512], F32), ps("pb", [128, 512], F32)]
        pg = [ps("pg0", [128, 512], BF16), ps("pg1", [128, 512], BF16)]

        P = Prog(nc)
        cnt = {"cb": 0, "ub": 0, "vb": 0, "pab": 0, "d": 0}

        P.dma(idf[:], idf_d, writes=["idf"])
        P.dma(idb[:], idb_d, writes=["idb"])
        P.dma(k12f[:], k12_d, writes=["k12f"])
        P.op("dve", lambda e: e.tensor_copy(out=k12b[:], in_=k12f[:]), reads=["k12f"], writes=["k12b"])

        def next_pab():
            i = cnt["pab"] % 2
            cnt["pab"] += 1
            return i

        def layer_norm(src_res, gi, bi, dst, dst_res):
            for q in range(4):
                P.op("dve", lambda e, q=q: e.bn_stats(out=st6[:, q, :], in_=x1[:, q * 512:(q + 1) * 512]),
                     reads=[src_res], writes=[("st6", q)])
            P.op("dve", lambda e: e.bn_aggr(out=mv[:], in_=st6[:].rearrange("p a b -> p (a b)")),
                 reads=[("st6", q) for q in range(4)], writes=["mv"])
            P.op("act", lambda e: e.activation(out=rstd[:], in_=mv[:, 1:2], func=AF.Sqrt, bias=LN_EPS, scale=1.0),
                 reads=["mv"], writes=["rstd"])
            P.op("dve", lambda e: e.reciprocal(out=rstd[:], in_=rstd[:]), reads=["rstd"], writes=["rstd"])
            P.op("dve", lambda e: e.tensor_scalar(out=x1[:], in0=x1[:], scalar1=mv[:, 0:1], scalar2=rstd[:, 0:1],
                                                  op0=ALU.subtract, op1=ALU.mult),
                 reads=[src_res, "mv", "rstd"], writes=[src_res])
            P.dma(lnb[:], lnp_d[gi:gi + 1, :].to_broadcast([128, 2048]), writes=["lnb"])
            P.op("dve", lambda e: e.tensor_tensor(out=x1[:], in0=x1[:], in1=lnb[:], op=ALU.mult),
                 reads=[src_res, "lnb"], writes=[src_res])
            P.dma(lnb[:], lnp_d[bi:bi + 1, :].to_broadcast([128, 2048]), writes=["lnb"])
            P.op("dve", lambda e: e.tensor_tensor(out=dst, in0=x1[:], in1=lnb[:], op=ALU.add),
                 reads=[src_res, "lnb"], writes=[dst_res])

        for ti in range(NT):
            tsl = slice(ti * 128, (ti + 1) * 128)
            mb = 0
            P.dma(mt[mb][:], mT[:, :, tsl].rearrange("k p t -> p k t"), writes=[("mt", mb)])
            P.dma(xr[:], xres[tsl, :], writes=["xr"])
            for kc in range(KC):
                b = cnt["cb"] % NCB
                cnt["cb"] += 1
                P.dma(cb[b][:], wo[kc], writes=[("cb", b)])
                for dq in range(4):
                    P.op("pe", lambda e, kc=kc, dq=dq, b=b, mb=mb: e.matmul(
                        acc4[:, dq * 512:(dq + 1) * 512], lhsT=mt[mb][:, kc, :], rhs=cb[b][:, dq * 512:(dq + 1) * 512],
                        start=(kc == 0), stop=(kc == KC - 1)),
                        reads=[("mt", mb), ("cb", b)], writes=["acc4"])
            P.op("dve", lambda e: e.scalar_tensor_tensor(out=x1[:], in0=xr[:], scalar=DN_ALPHA, in1=acc4[:],
                                                         op0=ALU.mult, op1=ALU.add),
                 reads=["xr", "acc4"], writes=["x1"])
            layer_norm("x1", 0, 1, x1[:], "x1")
            for g4 in range(4):
                pi = next_pab()
                for k in range(4):
                    dc = g4 * 4 + k
                    P.op("pe", lambda e, dc=dc, k=k, pi=pi: e.transpose(
                        out=pab[pi][:, k * 128:(k + 1) * 128], in_=x1[:, dc * 128:(dc + 1) * 128], identity=idf[:]),
                        reads=["x1", "idf"], writes=[("pab", pi)])
                P.op("act", lambda e, g4=g4, pi=pi: e.activation(
                    out=x1T[:, g4 * 4:(g4 + 1) * 4, :], in_=pab[pi][:].rearrange("p (a b) -> p a b", a=4), func=AF.Copy),
                    reads=[("pab", pi)], writes=["x1T"])
            for dc in range(16):
                b = cnt["cb"] % NCB
                cnt["cb"] += 1
                P.dma(cb[b][:], wq[dc], writes=[("cb", b)])
                for dq in range(4):
                    P.op("pe", lambda e, dc=dc, dq=dq, b=b: e.matmul(
                        acc4[:, dq * 512:(dq + 1) * 512], lhsT=x1T[:, dc, :], rhs=cb[b][:, dq * 512:(dq + 1) * 512],
                        start=(dc == 0), stop=(dc == 15)),
                        reads=["x1T", ("cb", b)], writes=["acc4"])
            P.op("act", lambda e: e.activation(out=qsb[:], in_=acc4[:], func=AF.Copy), reads=["acc4"], writes=["xr"])
            for g4 in range(4):
                pi = next_pab()
                for k in range(4):
                    j = g4 * 4 + k
                    P.op("pe", lambda e, j=j, k=k, pi=pi: e.transpose(
                        out=pab[pi][:, k * 128:(k + 1) * 128], in_=qsb[:, j * 128:(j + 1) * 128], identity=idf[:]),
                        reads=["xr", "idf"], writes=[("pab", pi)])
                P.op("act", lambda e, g4=g4, pi=pi: e.activation(
                    out=qT[:, g4 * 4:(g4 + 1) * 4, :], in_=pab[pi][:].rearrange("p (a b) -> p a b", a=4), func=AF.Copy),
                    reads=[("pab", pi)], writes=["qT"])
            for half, sdst, sres in ((0, s1, "s1"), (1, s2, "s2")):
                for g2 in range(2):
                    pi = next_pab()
                    for k in range(4):
                        h = g2 * 4 + k
                        P.op("pe", lambda e, h=h, k=k, pi=pi, half=half: e.matmul(
                            pab[pi][:, k * 128:(k + 1) * 128], lhsT=qT[:, 2 * h + half, :], rhs=k12b[:, half, :],
                            start=True, stop=True),
                            reads=["qT", "k12b"], writes=[("pab", pi)])
                    P.op("act", lambda e, g2=g2, pi=pi, sdst=sdst: e.activation(
                        out=sdst[:, g2 * 4:(g2 + 1) * 4, :], in_=pab[pi][:].rearrange("p (a b) -> p a b", a=4), func=AF.Copy),
                        reads=[("pab", pi)], writes=[sres])
            for sdst, sres, mm, mres in ((s1, "s1", m1, "m1"), (s2, "s2", m2, "m2")):
                for h in range(8):
                    P.op("dve", lambda e, h=h, sdst=sdst, mm=mm: e.max(out=mm[:, h, 0:8], in_=sdst[:, h, :]),
                         reads=[sres], writes=[mres])
                    P.op("dve", lambda e, h=h, sdst=sdst, mm=mm: e.match_replace(
                        out=tmp[:, 0:128], in_to_replace=mm[:, h, 0:8], in_values=sdst[:, h, :], imm_value=NEG),
                        reads=[sres, mres], writes=["tmp"])
                    P.op("dve", lambda e, h=h, mm=mm: e.max(out=mm[:, h, 8:16], in_=tmp[:, 0:128]),
                         reads=["tmp"], writes=[mres])
            for h in range(8):
                P.op("dve", lambda e, h=h: e.tensor_tensor(
                    out=cand[:], in0=m1[:, h, :].unsqueeze(2).to_broadcast([128, 16, 16]),
                    in1=m2[:, h, :].unsqueeze(1).to_broadcast([128, 16, 16]), op=ALU.add),
                    reads=["m1", "m2"], writes=["cand"])
                P.op("dve", lambda e, h=h: e.max(out=sc16[:, h, 0:8], in_=cand[:].rearrange("p a b -> p (a b)")),
                     reads=["cand"], writes=["sc16"])
                P.op("dve", lambda e, h=h: e.match_replace(
                    out=tmp[:], in_to_replace=sc16[:, h, 0:8], in_values=cand[:].rearrange("p a b -> p (a b)"),
                    imm_value=NEG), reads=["cand", "sc16"], writes=["tmp"])
                P.op("dve", lambda e, h=h: e.max(out=sc16[:, h, 8:16], in_=tmp[:]),
                     reads=["tmp"], writes=["sc16"])
                P.op("dve", lambda e, h=h: e.tensor_scalar(out=ex[:, h, :], in0=sc16[:, h, :], scalar1=sc16[:, h, 15:16],
                                                           scalar2=None, op0=ALU.subtract),
                     reads=["sc16"], writes=["ex"])
                P.op("dve", lambda e, h=h: e.tensor_scalar(out=s1[:, h, :], in0=s1[:, h, :], scalar1=sc16[:, h, 15:16],
                                                           scalar2=None, op0=ALU.subtract),
                     reads=["sc16", "s1"], writes=["s1"])
            P.op("act", lambda e: e.activation(out=ex[:], in_=ex[:], func=AF.Exp), reads=["ex"], writes=["ex"])
            P.op("dve", lambda e: e.tensor_reduce(out=zz[:], in_=ex[:], axis=AX.X, op=ALU.add),
                 reads=["ex"], writes=["zz"])
            P.op("act", lambda e: e.activation(out=zz[:], in_=zz[:], func=AF.Ln), reads=["zz"], writes=["zz"])
            P.op("dve", lambda e: e.tensor_scalar(out=nlnz[:], in0=zz[:], scalar1=-1.0, scalar2=None, op0=ALU.mult),
                 reads=["zz"], writes=["nlnz"])
            for h in range(8):
                for gq in range(128 // GE):
                    di = cnt["d"] % 2
                    cnt["d"] += 1
                    esl = slice(gq * GE, (gq + 1) * GE)
                    P.op("dve", lambda e, h=h, esl=esl, di=di: e.tensor_tensor(
                        out=Db[di][:], in0=s1[:, h, esl].unsqueeze(2).to_broadcast([128, GE, 128]),
                        in1=s2[:, h, :].unsqueeze(1).to_broadcast([128, GE, 128]), op=ALU.add),
                        reads=["s1", "s2"], writes=[("D", di)])
                    P.op("act", lambda e, h=h, di=di: e.activation(
                        out=Eb[di][:], in_=Db[di][:], func=AF.Exp, bias=nlnz[:, h:h + 1], scale=1.0),
                        reads=[("D", di), "nlnz"], writes=[("E", di)])
                    if h == 0:
                        P.op("dve", lambda e, esl=esl, di=di: e.scalar_tensor_tensor(
                            out=G[:, esl, :], in0=Db[di][:], scalar=0.0, in1=Eb[di][:], op0=ALU.is_ge, op1=ALU.mult),
                            reads=[("D", di), ("E", di)], writes=[("G", gq)])
                    else:
                        P.op("dve", lambda e, di=di: e.scalar_tensor_tensor(
                            out=Gh[di][:], in0=Db[di][:], scalar=0.0, in1=Eb[di][:], op0=ALU.is_ge, op1=ALU.mult),
                            reads=[("D", di), ("E", di)], writes=[("Gh", di)])
                        P.op("dve", lambda e, esl=esl, di=di: e.tensor_tensor(
                            out=G[:, esl, :], in0=G[:, esl, :], in1=Gh[di][:], op=ALU.add),
                            reads=[("Gh", di), ("G", gq)], writes=[("G", gq)])
            for eg in range(32):
                u_i = cnt["ub"] % NUB
                cnt["ub"] += 1
                v_i = cnt["vb"] % NVB
                cnt["vb"] += 1
                P.dma(ub[u_i][:].rearrange("p a b -> p (a b)"), uT[eg], writes=[("ub", u_i)])
                P.dma(vb[v_i][:].rearrange("p a b -> p (a b)"), vv[eg], writes=[("vb", v_i)])
                pi = eg % 2
                for dc in range(16):
                    P.op("pe", lambda e, dc=dc, pi=pi, u_i=u_i: e.matmul(
                        pab[pi][:], lhsT=x1T[:, dc, :], rhs=ub[u_i][:, dc, :], start=(dc == 0), stop=(dc == 15)),
                        reads=[("ub", u_i), "x1T"], writes=[("pab", pi)])
                P.op("act", lambda e, pi=pi: e.activation(out=gel[pi][:], in_=pab[pi][:], func=AF.Gelu),
                     reads=[("pab", pi)], writes=[("gel", pi)])
                P.op("dve", lambda e, pi=pi, eg=eg: e.tensor_tensor(
                    out=wsb[pi][:], in0=gel[pi][:], in1=G[:, eg * 4:(eg + 1) * 4, :].rearrange("p a b -> p (a b)"), op=ALU.mult),
                    reads=[("gel", pi)] + [("G", (eg * 4 + j) // GE) for j in range(4)], writes=[("wsb", pi)])
                for j in range(4):
                    P.op("pe", lambda e, pi=pi, j=j: e.transpose(
                        out=pg[pi][:, j * 128:(j + 1) * 128], in_=wsb[pi][:, j * 128:(j + 1) * 128], identity=idb[:]),
                        reads=[("wsb", pi), "idb"], writes=[("pg", pi)])
                P.op("act", lambda e, pi=pi: e.activation(
                    out=wt[pi][:], in_=pg[pi][:].rearrange("p (a b) -> p a b", a=4), func=AF.Copy),
                    reads=[("pg", pi)], writes=[("wt", pi)])
                for j in range(4):
                    for dq in range(4):
                        P.op("pe", lambda e, eg=eg, j=j, dq=dq, pi=pi, v_i=v_i: e.matmul(
                            acc4[:, dq * 512:(dq + 1) * 512], lhsT=wt[pi][:, j, :], rhs=vb[v_i][:, j, dq * 512:(dq + 1) * 512],
                            start=(eg == 0 and j == 0), stop=(eg == 31 and j == 3)),
                            reads=[("wt", pi), ("vb", v_i)], writes=["acc4"])
            P.op("dve", lambda e: e.scalar_tensor_tensor(out=x1[:], in0=x1[:], scalar=DN_ALPHA, in1=acc4[:],
                                                         op0=ALU.mult, op1=ALU.add),
                 reads=["x1", "acc4"], writes=["x1"])
            layer_norm("x1", 2, 3, xr[:], "xr")
            P.dma(out[tsl, :], xr[:], reads=["xr"], queue="pool")
            P.op("act", lambda e: e.activation(out=cb[0][:], in_=xr[:], func=AF.Copy), reads=["xr"], writes=[("cb", 0)])
            P.dma(outb[tsl, :], cb[0][:], reads=[("cb", 0)], queue="pool")
        P.emit()
    return nc


NEG = -1.0e30


def build_fox(S, NH):
    nc = bass.Bass("TRN2", target_bir_lowering=False)
    NB = S // 512
    NQ = S // 128
    NG = S // 512
    xT = nc.dram_tensor("xT", [16, 128, S], BF16, kind="ExternalInput").ap()
    wqk_d = nc.dram_tensor("wqk", [NH, 128, 16, 256], BF16, kind="ExternalInput").ap()
    wv_d = nc.dram_tensor("wv", [128, 16, NH * 128], BF16, kind="ExternalInput").ap()
    wf_d = nc.dram_tensor("wf", [128, 16, NH], BF16, kind="ExternalInput").ap()
    bf_d = nc.dram_tensor("bf", [NH, 1], F32, kind="ExternalInput").ap()
    mk_d = nc.dram_tensor("maskT", [128, 128], BF16, kind="ExternalInput").ap()
    idb_d = nc.dram_tensor("identb", [128, 128], BF16, kind="ExternalInput").ap()
    idf_d = nc.dram_tensor("identf", [128, 128], F32, kind="ExternalInput").ap()
    oT = nc.dram_tensor("oT", [NH, 128, S], BF16, kind="ExternalOutput").ap()
    negc_d = nc.dram_tensor("negc_scr", [NH, S], F32).ap()
    bsc_d = nc.dram_tensor("b_scr", [NH, 1], F32).ap()

    with contextlib.ExitStack() as st:
        def sb(name, shape, dt):
            return st.enter_context(nc.sbuf_tensor(name, shape, dt))

        def ps(name, shape, dt):
            return st.enter_context(nc.psum_tensor(name, shape, dt))

        xb = [sb("xb%d" % i, [128, 16, 512], BF16) for i in range(2)]
        w = sb("w", [128, 16, 256], BF16)
        wv = sb("wv_s", [128, 16, NH * 128], BF16)
        wf = sb("wf_s", [128, 16, NH], BF16)
        bfs = sb("bfs", [NH, 1], F32)
        nbf = sb("nbf", [NH, 1], F32)
        maskT = sb("maskT_s", [128, 128], BF16)
        idb = sb("idb", [128, 128], BF16)
        idf = sb("idf", [128, 128], F32)
        ones_b = sb("ones_b", [128, 1], BF16)
        ones_f = sb("ones_f", [1, 128], F32)
        qT = sb("qT", [128, S], BF16)
        kT = sb("kT", [128, S], BF16)
        v = sb("v", [128, NQ, NH * 128], BF16)
        lsp = sb("lsp", [NH, 512], F32)
        ngb = [sb("ngb%d" % i, [NH, 512], F32) for i in range(2)]
        sq = sb("sq", [128, 512], BF16)
        qmx = sb("qmx", [1, NB], F32)
        kmx = sb("kmx", [1, NB], F32)
        q2 = sb("q2", [1, 1], F32)
        k2 = sb("k2", [1, 1], F32)
        Bb = sb("Bb", [128, 1], F32)
        refb = sb("refb", [128, NG], F32)
        ncr = sb("ncr", [NQ, 128], F32)
        negc_col = sb("negc_col", [128, NQ], F32)
        biasmat = sb("biasmat", [128, NG, NQ], F32)
        pT = [sb("pT%d" % i, [128, 512], BF16) for i in range(2)]
        rrow = sb("rrow", [1, 512], F32)
        osb = sb("osb", [128, 512], F32)
        oTs = sb("oTs", [128, S], BF16)

        pq = [ps("pq%d" % i, [128, 512], F32) for i in range(4)]
        po = ps("po", [128, 512], F32)
        psm = ps("psm", [128, 512], F32)
        pbc = ps("pbc", [128, 512], F32)
        pn = ps("pn", [128, 512], F32)

        P = Prog(nc)
        cnt = {"pq": 0, "x": 0, "pt": 0}

        def nb():
            i = cnt["pq"] % 4
            cnt["pq"] += 1
            return i

        P.dma(wv[:], wv_d, writes=["wv"])
        P.dma(wf[:], wf_d, writes=["wf"])
        P.dma(bfs[:], bf_d, writes=["bfs"])
        P.dma(maskT[:], mk_d, writes=["maskT"])
        P.dma(idb[:], idb_d, writes=["idb"])
        P.dma(idf[:], idf_d, writes=["idf"])
        P.op("dve", lambda e: e.tensor_scalar(out=nbf[:], in0=bfs[:], scalar1=-1.0, scalar2=None, op0=ALU.mult),
             reads=["bfs"], writes=["nbf"])
        P.op("pool", lambda e: e.memset(ones_b[:], 1.0), writes=["ones_b"])
        P.op("pool", lambda e: e.memset(ones_f[:], 1.0), writes=["ones_f"])

        def load_x(blk):
            xi = cnt["x"] % 2
            cnt["x"] += 1
            P.dma(xb[xi][:], xT[:, :, blk * 512:(blk + 1) * 512].rearrange("k p t -> p k t"), writes=[("xb", xi)])
            return xi

        for blk in range(NB):
            xi = load_x(blk)
            bsl = slice(blk * 512, (blk + 1) * 512)
            for tt in range(4):
                pi = nb()
                for dc in range(16):
                    P.op("pe", lambda e, dc=dc, pi=pi, xi=xi, tt=tt: e.matmul(
                        pq[pi][:, 0:NH * 128], lhsT=xb[xi][:, dc, tt * 128:(tt + 1) * 128], rhs=wv[:, dc, :],
                        start=(dc == 0), stop=(dc == 15)),
                        reads=["wv", ("xb", xi)], writes=[("pq", pi)])
                eng = "act" if tt % 2 == 0 else "dve"
                if eng == "act":
                    P.op("act", lambda e, pi=pi, blk=blk, tt=tt: e.activation(
                        out=v[:, blk * 4 + tt, :], in_=pq[pi][:, 0:NH * 128], func=AF.Copy),
                        reads=[("pq", pi)], writes=["v"])
                else:
                    P.op("dve", lambda e, pi=pi, blk=blk, tt=tt: e.tensor_copy(
                        out=v[:, blk * 4 + tt, :], in_=pq[pi][:, 0:NH * 128]),
                        reads=[("pq", pi)], writes=["v"])
            for dc in range(16):
                P.op("pe", lambda e, dc=dc, xi=xi: e.matmul(
                    pn[0:NH, :], lhsT=wf[:, dc, :], rhs=xb[xi][:, dc, :], start=(dc == 0), stop=(dc == 15)),
                    reads=["wf", ("xb", xi)], writes=["pn"])
            P.op("act", lambda e: e.activation(out=lsp[:], in_=pn[0:NH, :], func=AF.Exp, bias=nbf[:, 0:1], scale=-1.0),
                 reads=["pn", "nbf"], writes=["lsp"])
            P.op("act", lambda e: e.activation(out=lsp[:], in_=lsp[:], func=AF.Ln, bias=1.0, scale=1.0),
                 reads=["lsp"], writes=["lsp"])
            gi = blk % 2
            if blk == 0:
                P.op("dve", lambda e, gi=gi: e.tensor_tensor_scan(
                    out=ngb[gi][:], data0=lsp[:], data1=lsp[:], initial=0.0, op0=ALU.add, op1=ALU.max),
                    reads=["lsp"], writes=[("ngb", gi)])
            else:
                P.op("dve", lambda e, gi=gi: e.tensor_tensor_scan(
                    out=ngb[gi][:], data0=lsp[:], data1=lsp[:], initial=ngb[1 - gi][:, 511:512], op0=ALU.add, op1=ALU.max),
                    reads=["lsp", ("ngb", 1 - gi)], writes=[("ngb", gi)])
            P.dma(negc_d[:, bsl], ngb[gi][:], reads=[("ngb", gi)], writes=["negc_d"], queue="pool")

        for h in range(NH):
            P.dma(w[:], wqk_d[h], writes=["w"])
            for blk in range(NB):
                xi = load_x(blk)
                bsl = slice(blk * 512, (blk + 1) * 512)
                for which, dst, dres, scale, mxt, mres in ((0, qT, "qT", 128 ** -0.5, qmx, "qmx"), (1, kT, "kT", 1.0, kmx, "kmx")):
                    pi = nb()
                    for dc in range(16):
                        P.op("pe", lambda e, dc=dc, pi=pi, xi=xi, which=which: e.matmul(
                            pq[pi][:], lhsT=w[:, dc, which * 128:(which + 1) * 128], rhs=xb[xi][:, dc, :],
                            start=(dc == 0), stop=(dc == 15)),
                            reads=["w", ("xb", xi)], writes=[("pq", pi)])
                    P.op("act", lambda e, pi=pi, dst=dst, bsl=bsl, scale=scale: e.activation(
                        out=dst[:, bsl], in_=pq[pi][:], func=AF.Copy, scale=scale),
                        reads=[("pq", pi)], writes=[dres])
                    P.op("dve", lambda e, dst=dst, bsl=bsl: e.tensor_tensor(out=sq[:], in0=dst[:, bsl], in1=dst[:, bsl], op=ALU.mult),
                         reads=[dres], writes=["sq"])
                    P.op("pe", lambda e: e.matmul(pn[0:1, :], lhsT=ones_b[:], rhs=sq[:], start=True, stop=True),
                         reads=["ones_b", "sq"], writes=["pn"])
                    P.op("dve", lambda e, mxt=mxt, blk=blk: e.tensor_reduce(
                        out=mxt[:, blk:blk + 1], in_=pn[0:1, :], axis=AX.X, op=ALU.max),
                        reads=["pn"], writes=[mres])
            P.op("dve", lambda e: e.tensor_reduce(out=q2[:], in_=qmx[:], axis=AX.X, op=ALU.max), reads=["qmx"], writes=["q2"])
            P.op("dve", lambda e: e.tensor_reduce(out=k2[:], in_=kmx[:], axis=AX.X, op=ALU.max), reads=["kmx"], writes=["k2"])
            P.op("dve", lambda e: e.tensor_tensor(out=q2[:], in0=q2[:], in1=k2[:], op=ALU.mult), reads=["q2", "k2"], writes=["q2"])
            P.op("act", lambda e: e.activation(out=q2[:], in_=q2[:], func=AF.Sqrt, scale=1.1025), reads=["q2"], writes=["q2"])
            P.dma(bsc_d[h:h + 1, :], q2[:], reads=["q2"], writes=[("bsc", h)], queue="pool")
            P.dma(Bb[:], bsc_d[h:h + 1, :].to_broadcast([128, 1]), reads=[("bsc", h)], writes=["Bb"])
            P.dma(refb[:], negc_d[h:h + 1, :].rearrange("o (g c) -> o g c", c=512)[:, :, 511].to_broadcast([128, NG]),
                  reads=["negc_d"], writes=["refb"], allow_slow_non_contiguous=True)
            P.op("dve", lambda e: e.tensor_scalar(out=refb[:], in0=refb[:], scalar1=Bb[:, 0:1], scalar2=None, op0=ALU.add),
                 reads=["refb", "Bb"], writes=["refb"])
            P.dma(ncr[:], negc_d[h:h + 1, :].rearrange("o (k p) -> (o k) p", p=128), reads=["negc_d"], writes=["ncr"])
            P.op("pe", lambda e: e.transpose(out=pn[:, 0:NQ], in_=ncr[:], identity=idf[0:NQ, 0:NQ]),
                 reads=["ncr", "idf"], writes=["pn"])
            P.op("dve", lambda e: e.tensor_copy(out=negc_col[:], in_=pn[:, 0:NQ]), reads=["pn"], writes=["negc_col"])
            for qg in range(NG):
                P.op("dve", lambda e, qg=qg: e.tensor_scalar(
                    out=biasmat[:, qg, :], in0=negc_col[:], scalar1=refb[:, qg:qg + 1], scalar2=None, op0=ALU.subtract),
                    reads=["negc_col", "refb"], writes=["biasmat"])
            for qg in range(NG):
                q0 = qg * 512
                nkb = 4 * qg + 4
                for kb in range(nkb):
                    off = max(0, kb * 128 - q0)
                    N = 512 - off
                    diag = kb >= 4 * qg
                    pi = nb()
                    ti = cnt["pt"] % 2
                    cnt["pt"] += 1
                    P.op("pe", lambda e, pi=pi, kb=kb, q0=q0, off=off, N=N, diag=diag: e.matmul(
                        pq[pi][:, 0:N], lhsT=kT[:, kb * 128:(kb + 1) * 128], rhs=qT[:, q0 + off:q0 + 512],
                        start=True, stop=(not diag)),
                        reads=["qT", "kT"], writes=[("pq", pi)])
                    if diag:
                        P.op("pe", lambda e, pi=pi: e.matmul(
                            pq[pi][:, 0:128], lhsT=idb[:], rhs=maskT[:], start=False, stop=True),
                            reads=["idb", "maskT"], writes=[("pq", pi)])
                    P.op("act", lambda e, pi=pi, ti=ti, N=N, qg=qg, kb=kb: e.activation(
                        out=pT[ti][:, 0:N], in_=pq[pi][:, 0:N], func=AF.Exp, bias=biasmat[:, qg, kb:kb + 1], scale=1.0),
                        reads=[("pq", pi), "biasmat"], writes=[("pT", ti)])
                    P.op("pe", lambda e, ti=ti, kb=kb, off=off, N=N, nkb=nkb, h=h: e.matmul(
                        po[:, off:512], lhsT=v[:, kb, h * 128:(h + 1) * 128], rhs=pT[ti][:, 0:N],
                        start=(kb == 0), stop=(kb == nkb - 1)),
                        reads=["v", ("pT", ti)], writes=["po"])
                    P.op("pe", lambda e, ti=ti, kb=kb, off=off, N=N, nkb=nkb: e.matmul(
                        psm[0:1, off:512], lhsT=ones_b[:], rhs=pT[ti][:, 0:N], start=(kb == 0), stop=(kb == nkb - 1)),
                        reads=["ones_b", ("pT", ti)], writes=["psm"])
                P.op("dve", lambda e: e.reciprocal(out=rrow[:], in_=psm[0:1, :]), reads=["psm"], writes=["rrow"])
                P.op("pe", lambda e: e.matmul(pbc[:], lhsT=ones_f[:], rhs=rrow[:], start=True, stop=True),
                     reads=["ones_f", "rrow"], writes=["pbc"])
                P.op("act", lambda e: e.activation(out=osb[:], in_=po[:], func=AF.Copy), reads=["po"], writes=["osb"])
                P.op("dve", lambda e, q0=q0: e.tensor_tensor(out=oTs[:, q0:q0 + 512], in0=osb[:], in1=pbc[:], op=ALU.mult),
                     reads=["osb", "pbc"], writes=["oTs"])
            P.dma(oT[h], oTs[:], reads=["oTs"], queue="pool")
        P.emit()
    return nc


LN_EPS = 1e-5


def build_ret(S, NH):
    nc = bass.Bass("TRN2", target_bir_lowering=False)
    NB = S // 512
    NQ = S // 128
    NG = S // 512
    xT = nc.dram_tensor("xT", [16, 128, S], BF16, kind="ExternalInput").ap()
    wqk_d = nc.dram_tensor("wqk", [NH, 128, 16, 512], BF16, kind="ExternalInput").ap()
    wv_d = nc.dram_tensor("wv", [NH, 128, 16, 512], BF16, kind="ExternalInput").ap()
    wg_d = nc.dram_tensor("wg", [NH, 128, 16, 512], BF16, kind="ExternalInput").ap()
    cos_d = nc.dram_tensor("cosT", [128, S], F32, kind="ExternalInput").ap()
    sin_d = nc.dram_tensor("sinT", [128, S], F32, kind="ExternalInput").ap()
    tz_d = nc.dram_tensor("tz", [NH, 128, S], BF16, kind="ExternalInput").ap()
    dg_d = nc.dram_tensor("dgm", [NH, 128, 128], BF16, kind="ExternalInput").ap()
    gn_d = nc.dram_tensor("gng", [NH, 512], F32, kind="ExternalInput").ap()
    mo = nc.dram_tensor("mo", [S, NH * 512], BF16, kind="ExternalOutput").ap()

    with contextlib.ExitStack() as st:
        def sb(name, shape, dt):
            return st.enter_context(nc.sbuf_tensor(name, shape, dt))

        def ps(name, shape, dt):
            return st.enter_context(nc.psum_tensor(name, shape, dt))

        xb = sb("xb", [128, 16, 512], BF16)
        wqk = sb("wqk_s", [128, 16, 512], BF16)
        assert S == 8192 or S <= 8192
        tzv = wqk[:].rearrange("p a b -> p (a b)")
        wvg = sb("wvg_s", [128, 16, 512], BF16)
        cs = sb("cs", [128, 512], F32)
        sn = sb("sn", [128, 512], F32)
        tt = [sb("tt%d" % i, [128, 512], F32) for i in range(2)]
        qT = [sb("qT%d" % i, [128, S], BF16) for i in range(2)]
        kT = [sb("kT%d" % i, [128, S], BF16) for i in range(2)]
        v = sb("v", [128, NQ, 512], BF16)
        dgm = sb("dgm_s", [128, 128], BF16)
        gng = sb("gng_s", [128, 512], F32)
        wT = [sb("wT%d" % i, [128, 512], BF16) for i in range(2)]
        st6 = sb("st6", [128, 6], F32)
        mv = sb("mv", [128, 2], F32)
        rstd = sb("rstd", [128, 1], F32)
        on = sb("on", [128, 512], F32)
        sg = sb("sg", [128, 512], F32)
        tt = tt + [on, sg]
        ttres = [("tt", 0), ("tt", 1), "on", "sg"]
        res = [sb("res%d" % i, [128, 512], BF16) for i in range(2)]

        pb = [ps("pb%d" % i, [128, 512], F32) for i in range(8)]

        P = Prog(nc)
        cnt = {"p": 0, "w": 0, "r": 0}

        def nb():
            i = cnt["p"] % 8
            cnt["p"] += 1
            return i

        for h in range(NH):
            P.dma(wqk[:], wqk_d[h], writes=["wqk"])
            P.dma(wvg[:], wv_d[h], writes=["wvg"])
            P.dma(dgm[:], dg_d[h], writes=["dgm"])
            P.dma(gng[:], gn_d[h:h + 1, :].to_broadcast([128, 512]), writes=["gng"])
            for blk in range(NB):
                bsl = slice(blk * 512, (blk + 1) * 512)
                P.dma(xb[:], xT[:, :, bsl].rearrange("k p t -> p k t"), writes=["xb"])
                P.dma(cs[:], cos_d[:, bsl], writes=["cs"])
                P.dma(sn[:], sin_d[:, bsl], writes=["sn"])
                for which, dst, dres, scale in ((0, qT, "qT", 1.0), (1, kT, "kT", 0.0625)):
                    pa = nb()
                    pc = nb()
                    for half, pi in ((0, pa), (1, pc)):
                        c0 = which * 256 + half * 128
                        for dc in range(16):
                            P.op("pe", lambda e, dc=dc, pi=pi, c0=c0: e.matmul(
                                pb[pi][:], lhsT=wqk[:, dc, c0:c0 + 128], rhs=xb[:, dc, :], start=(dc == 0), stop=(dc == 15)),
                                reads=["wqk", "xb"], writes=[("pb", pi)])
                    for ti, (src, tab, tres) in enumerate(((pa, cs, "cs"), (pc, sn, "sn"), (pa, sn, "sn"), (pc, cs, "cs"))):
                        P.op("dve", lambda e, ti=ti, src=src, tab=tab, scale=scale: e.scalar_tensor_tensor(
                            out=tt[ti][:], in0=pb[src][:], scalar=scale, in1=tab[:], op0=ALU.mult, op1=ALU.mult),
                            reads=[("pb", src), tres], writes=[ttres[ti]])
                    P.op("dve", lambda e, dst=dst, bsl=bsl: e.tensor_tensor(out=dst[0][:, bsl], in0=tt[0][:], in1=tt[1][:], op=ALU.subtract),
                         reads=[ttres[0], ttres[1]], writes=[dres + "0"])
                    P.op("dve", lambda e, dst=dst, bsl=bsl: e.tensor_tensor(out=dst[1][:, bsl], in0=tt[2][:], in1=tt[3][:], op=ALU.add),
                         reads=[ttres[2], ttres[3]], writes=[dres + "1"])
                for t4 in range(4):
                    pi = nb()
                    for dc in range(16):
                        P.op("pe", lambda e, dc=dc, pi=pi, t4=t4: e.matmul(
                            pb[pi][:], lhsT=xb[:, dc, t4 * 128:(t4 + 1) * 128], rhs=wvg[:, dc, :], start=(dc == 0), stop=(dc == 15)),
                            reads=["wvg", "xb"], writes=[("pb", pi)])
                    P.op("act", lambda e, pi=pi, blk=blk, t4=t4: e.activation(out=v[:, blk * 4 + t4, :], in_=pb[pi][:], func=AF.Copy),
                         reads=[("pb", pi)], writes=["v"])
            P.dma(tzv[:, 0:S], tz_d[h], writes=["wqk"])
            P.dma(wvg[:], wg_d[h], writes=["wvg"])
            for qg in range(NG):
                q0 = qg * 512
                nkb = 4 * qg + 4
                po = [0, 1, 2, 3]
                for kb in range(nkb):
                    off = max(0, kb * 128 - q0)
                    N = 512 - off
                    diag = kb >= 4 * qg
                    pi = 4 + cnt["w"] % 2
                    wi = cnt["w"] % 2
                    cnt["w"] += 1
                    for c in range(2):
                        P.op("pe", lambda e, pi=pi, kb=kb, q0=q0, off=off, N=N, c=c: e.matmul(
                            pb[pi][:, 0:N], lhsT=kT[c][:, kb * 128:(kb + 1) * 128], rhs=qT[c][:, q0 + off:q0 + 512],
                            start=(c == 0), stop=(c == 1)),
                            reads=["qT0", "qT1", "kT0", "kT1"], writes=[("pb", pi)])
                    if diag:
                        P.op("dve", lambda e, pi=pi, wi=wi: e.tensor_tensor(
                            out=wT[wi][:, 0:128], in0=pb[pi][:, 0:128], in1=dgm[:], op=ALU.mult),
                            reads=[("pb", pi), "dgm"], writes=[("wT", wi)])
                        if N > 128:
                            P.op("dve", lambda e, pi=pi, wi=wi, N=N: e.tensor_tensor(
                                out=wT[wi][:, 128:N], in0=pb[pi][:, 128:N], in1=tzv[:, 128:N], op=ALU.mult),
                                reads=[("pb", pi), "wqk"], writes=[("wT", wi)])
                    else:
                        n0 = q0 - kb * 128
                        P.op("dve", lambda e, pi=pi, wi=wi, n0=n0: e.tensor_tensor(
                            out=wT[wi][:], in0=pb[pi][:], in1=tzv[:, n0:n0 + 512], op=ALU.mult),
                            reads=[("pb", pi), "wqk"], writes=[("wT", wi)])
                    for j in range(off // 128, 4):
                        tb = 4 * qg + j
                        c0 = j * 128 - off
                        P.op("pe", lambda e, wi=wi, kb=kb, j=j, c0=c0, tb=tb, po=po: e.matmul(
                            pb[po[j]][:], lhsT=wT[wi][:, c0:c0 + 128], rhs=v[:, kb, :], start=(kb == 0), stop=(kb == tb)),
                            reads=[("wT", wi), "v"], writes=[("pb", po[j])])
                P.dma(xb[:], xT[:, :, q0:q0 + 512].rearrange("k p t -> p k t"), writes=["xb"])
                for j in range(4):
                    pg = 6 + j % 2
                    for dc in range(16):
                        P.op("pe", lambda e, dc=dc, pg=pg, j=j: e.matmul(
                            pb[pg][:], lhsT=xb[:, dc, j * 128:(j + 1) * 128], rhs=wvg[:, dc, :], start=(dc == 0), stop=(dc == 15)),
                            reads=["wvg", "xb"], writes=[("pb", pg)])
                    P.op("act", lambda e, pg=pg: e.activation(out=sg[:], in_=pb[pg][:], func=AF.Silu),
                         reads=[("pb", pg)], writes=["sg"])
                    pj = po[j]
                    P.op("act", lambda e, pj=pj: e.activation(out=on[:], in_=pb[pj][:], func=AF.Copy), reads=[("pb", pj)], writes=["on"])
                    P.op("dve", lambda e: e.bn_stats(out=st6[:], in_=on[:]), reads=["on"], writes=["st6"])
                    P.op("dve", lambda e: e.bn_aggr(out=mv[:], in_=st6[:]), reads=["st6"], writes=["mv"])
                    P.op("act", lambda e: e.activation(out=rstd[:], in_=mv[:, 1:2], func=AF.Sqrt, bias=LN_EPS, scale=1.0),
                         reads=["mv"], writes=["rstd"])
                    P.op("dve", lambda e: e.reciprocal(out=rstd[:], in_=rstd[:]), reads=["rstd"], writes=["rstd"])
                    P.op("dve", lambda e: e.tensor_scalar(out=on[:], in0=on[:], scalar1=mv[:, 0:1], scalar2=rstd[:, 0:1],
                                                          op0=ALU.subtract, op1=ALU.mult),
                         reads=["on", "mv", "rstd"], writes=["on"])
                    P.op("dve", lambda e: e.tensor_tensor(out=on[:], in0=on[:], in1=gng[:], op=ALU.mult),
                         reads=["on", "gng"], writes=["on"])
                    ri = cnt["r"] % 2
                    cnt["r"] += 1
                    P.op("dve", lambda e, ri=ri: e.tensor_tensor(out=res[ri][:], in0=on[:], in1=sg[:], op=ALU.mult),
                         reads=["on", "sg"], writes=[("res", ri)])
                    t0 = q0 + j * 128
                    P.dma(mo[t0:t0 + 128, h * 512:(h + 1) * 512], res[ri][:], reads=[("res", ri)], queue="pool")
        P.emit()
    return nc


def build_cast(F, TW=4096):
    nc = bass.Bass("TRN2", target_bir_lowering=False)
    x = nc.dram_tensor("x", [128, F], F32, kind="ExternalInput").ap()
    y = nc.dram_tensor("y", [128, F], BF16, kind="ExternalOutput").ap()
    with contextlib.ExitStack() as st:
        NBUF = 3
        xin = [st.enter_context(nc.sbuf_tensor("xin%d" % i, [128, TW], F32)) for i in range(NBUF)]
        yo = [st.enter_context(nc.sbuf_tensor("yo%d" % i, [128, TW], BF16)) for i in range(NBUF)]
        P = Prog(nc)
        for i in range(F // TW):
            b = i % NBUF
            P.dma(xin[b][:], x[:, i * TW:(i + 1) * TW], writes=[("xin", b)])
            if i % 2 == 0:
                P.op("dve", lambda e, b=b: e.tensor_copy(out=yo[b][:], in_=xin[b][:]), reads=[("xin", b)], writes=[("yo", b)])
            else:
                P.op("act", lambda e, b=b: e.activation(out=yo[b][:], in_=xin[b][:], func=AF.Copy),
                     reads=[("xin", b)], writes=[("yo", b)])
            P.dma(y[:, i * TW:(i + 1) * TW], yo[b][:], reads=[("yo", b)], queue="pool")
        P.emit()
    return nc


N_CORES = 8
SEQ = 8192
DM = 2048
CORES = list(range(N_CORES))


def _c(a):
    return np.ascontiguousarray(a)


def _chunk_rows(w):
    return _c(w.reshape(w.shape[0] // 128, 128, w.shape[1]))


def _pdc(w):
    return _c(w.reshape(16, 128, w.shape[1]).transpose(1, 0, 2))


def _run(nc, in_maps):
    res = run_bass_kernel_spmd(nc, in_maps, core_ids=CORES)
    return res.results


def _post_inputs(mT_list, wo_b, xres_list, lnp, wq_b, k1, k2, u_b, v_b):
    import ml_dtypes
    bf = ml_dtypes.bfloat16
    KC = wo_b.shape[0] // 128
    wo_c = np.zeros((32, 128, DM), bf)
    wo_c[:KC] = wo_b.reshape(KC, 128, DM)
    wq_c = _chunk_rows(wq_b)
    k12 = _c(np.stack([k1.T, k2.T], axis=1)).astype(np.float32)
    uT_c = _c(u_b.reshape(32, 512, 16, 128).transpose(0, 3, 2, 1)).reshape(32, 128, 8192)
    vv_c = _c(v_b.reshape(32, 4, 128, DM).transpose(0, 2, 1, 3)).reshape(32, 128, 8192)
    idf = np.eye(128, dtype=np.float32)
    idb = np.eye(128).astype(bf)
    maps = []
    for j in range(N_CORES):
        mT = np.zeros((32, 128, 2048), bf)
        m = mT_list[j]
        mT[:m.shape[0]] = m
        maps.append({"mT": mT, "wo": wo_c, "xres": _c(xres_list[j]), "lnp": lnp, "wq": wq_c, "k12T": k12,
                     "identf": idf, "identb": idb, "uT": uT_c, "vv": vv_c})
    return maps


def kernel(**inp):
    import ml_dtypes
    bf = ml_dtypes.bfloat16
    f32 = np.float32
    x = np.asarray(inp["x"], f32)

    names = ["x", "l0_fox_w_in", "l0_fox_w_o", "l0_peer_wq", "l0_peer_u", "l0_peer_v",
             "l1_ret_w_in", "l1_ret_w_o", "l1_peer_wq", "l1_peer_u", "l1_peer_v"]
    sizes = [int(np.asarray(inp[n]).size) for n in names]
    total = sum(sizes)
    TW = 4096
    unit = N_CORES * 128 * TW
    padded = ((total + unit - 1) // unit) * unit
    flat = np.zeros(padded, f32)
    o = 0
    for n, s in zip(names, sizes):
        flat[o:o + s] = np.asarray(inp[n], f32).reshape(-1)
        o += s
    F = padded // (N_CORES * 128)
    flat = flat.reshape(N_CORES, 128, F)
    res = _run(build_cast(F, TW), [{"x": flat[i]} for i in range(N_CORES)])
    del flat
    fb = np.concatenate([np.asarray(r["y"]).reshape(-1) for r in res])
    cast = {}
    o = 0
    for n, s in zip(names, sizes):
        cast[n] = fb[o:o + s].reshape(np.asarray(inp[n]).shape)
        o += s
    del fb
    xb = cast["x"]

    idf = np.eye(128, dtype=f32)
    idb = np.eye(128).astype(bf)
    kk = np.arange(128)

    w0 = cast["l0_fox_w_in"]
    maskT = np.where(kk[:, None] <= kk[None, :], 0.0, NEG).astype(f32).astype(bf)
    maps = []
    for c in range(N_CORES):
        b, hg = divmod(c, 4)
        xT = _c(xb[b].T).reshape(16, 128, SEQ)
        wqk = np.stack([np.concatenate([_pdc(w0[:, h * 128:(h + 1) * 128]),
                                        _pdc(w0[:, DM + h * 128:DM + (h + 1) * 128])], axis=-1)
                        for h in range(hg * 4, hg * 4 + 4)])
        maps.append({"xT": xT, "wqk": _c(wqk), "wv": _pdc(w0[:, 2 * DM + hg * 512:2 * DM + (hg + 1) * 512]),
                     "wf": _pdc(w0[:, 3 * DM + hg * 4:3 * DM + (hg + 1) * 4]),
                     "bf": _c(np.asarray(inp["l0_fox_b_f"], f32)[hg * 4:(hg + 1) * 4].reshape(4, 1)),
                     "maskT": maskT, "identb": idb, "identf": idf})
    r1 = _run(build_fox(SEQ, 4), maps)
    oT = [np.asarray(r["oT"]) for r in r1]

    post = build_post(4096, 16)
    mT_list, xres_list = [], []
    for j in range(N_CORES):
        b, tq = divmod(j, 4)
        tsl = slice(tq * 2048, (tq + 1) * 2048)
        mT_list.append(np.concatenate([oT[b * 4 + g][:, :, tsl] for g in range(4)], axis=0))
        xres_list.append(x[b, tsl, :])
    lnp0 = _c(np.stack([inp["l0_ln1_g"], inp["l0_ln1_b"], inp["l0_ln2_g"], inp["l0_ln2_b"]]).astype(f32))
    maps = _post_inputs(mT_list, cast["l0_fox_w_o"], xres_list, lnp0, cast["l0_peer_wq"],
                        np.asarray(inp["l0_peer_k1"], f32), np.asarray(inp["l0_peer_k2"], f32),
                        cast["l0_peer_u"], cast["l0_peer_v"])
    r2 = _run(post, maps)
    x1 = np.stack([np.concatenate([np.asarray(r2[b * 4 + t]["out"]) for t in range(4)], axis=0) for b in range(2)])
    x1b = np.stack([np.concatenate([np.asarray(r2[b * 4 + t]["outb"]) for t in range(4)], axis=0) for b in range(2)])
    del maps

    w1 = cast["l1_ret_w_in"]
    inv = (10000.0 ** (-np.arange(128, dtype=f32) / 128)).astype(f32)
    ang = np.arange(SEQ, dtype=f32)[None, :] * inv[:, None]
    cosT, sinT = np.cos(ang).astype(f32), np.sin(ang).astype(f32)
    pp = np.arange(128)[:, None]
    nn = np.arange(SEQ)[None, :]
    sl_, tl_ = np.arange(128)[:, None], np.arange(128)[None, :]
    same = (sl_ // 64) == (tl_ // 64)
    gn = np.asarray(inp["l1_ret_gn_g"], f32)
    maps = []
    for c in range(N_CORES):
        b, hp = divmod(c, 4)
        heads = [2 * hp, 2 * hp + 1]
        xT = _c(x1b[b].T).reshape(16, 128, SEQ)
        wqk = np.stack([np.concatenate([_pdc(w1[:, h * 256:(h + 1) * 256]),
                                        _pdc(w1[:, DM + h * 256:DM + (h + 1) * 256])], axis=-1) for h in heads])
        wv = np.stack([_pdc(w1[:, 2 * DM + h * 512:2 * DM + (h + 1) * 512]) for h in heads])
        wg = np.stack([_pdc(w1[:, 4 * DM + h * 512:4 * DM + (h + 1) * 512]) for h in heads])
        tz = np.zeros((2, 128, SEQ), f32)
        dgm = np.zeros((2, 128, 128), f32)
        for i, h in enumerate(heads):
            lg = np.log(1.0 - 2.0 ** (-5.0 - h))
            tz[i] = np.exp(lg * np.maximum(nn - pp, 0))
            dgm[i] = np.where(same, np.exp(lg * np.abs(tl_ - sl_)),
                              np.where(sl_ // 64 < tl_ // 64, np.exp(lg * (tl_ - sl_)), 0.0))
        maps.append({"xT": xT, "wqk": _c(wqk), "wv": _c(wv), "wg": _c(wg), "cosT": cosT, "sinT": sinT,
                     "tz": tz.astype(bf), "dgm": dgm.astype(bf),
                     "gng": _c(np.stack([gn[h * 512:(h + 1) * 512] for h in heads]))})
    r3 = _run(build_ret(SEQ, 2), maps)
    mo = [np.concatenate([np.asarray(r3[b * 4 + hp]["mo"]) for hp in range(4)], axis=1) for b in range(2)]
    del maps

    mT_list, xres_list = [], []
    for j in range(N_CORES):
        b, tq = divmod(j, 4)
        tsl = slice(tq * 2048, (tq + 1) * 2048)
        mT_list.append(_c(mo[b][tsl, :].T).reshape(32, 128, 2048))
        xres_list.append(x1[b, tsl, :])
    lnp1 = _c(np.stack([inp["l1_ln1_g"], inp["l1_ln1_b"], inp["l1_ln2_g"], inp["l1_ln2_b"]]).astype(f32))
    maps = _post_inputs(mT_list, cast["l1_ret_w_o"], xres_list, lnp1, cast["l1_peer_wq"],
                        np.asarray(inp["l1_peer_k1"], f32), np.asarray(inp["l1_peer_k2"], f32),
                        cast["l1_peer_u"], cast["l1_peer_v"])
    r4 = _run(post, maps)
    out = np.stack([np.concatenate([np.asarray(r4[b * 4 + t]["out"]) for t in range(4)], axis=0) for b in range(2)])
    return out.astype(f32)
```

```python
import contextlib
import numpy as np
import concourse.bass as bass
import concourse.mybir as mybir
from concourse.bass_utils import run_bass_kernel_spmd

F32 = mybir.dt.float32
BF16 = mybir.dt.bfloat16
AF = mybir.ActivationFunctionType
ALU = mybir.AluOpType
AX = mybir.AxisListType

ENGS = ("pe", "act", "dve", "pool", "sp")
N_DMA_SEMS = 40


class _Ins:
    __slots__ = ("eng", "fn", "deps", "inc", "dma", "idx")

    def __init__(self, eng, fn, dma=None):
        self.eng = eng
        self.fn = fn
        self.deps = []
        self.inc = False
        self.dma = dma
        self.idx = None


class Prog:
    def __init__(self, nc):
        self.nc = nc
        self.streams = {e: [] for e in ENGS}
        self.last_w = {}
        self.readers = {}
        self.known = {e: {} for e in ENGS}
        self.known_dma = {e: set() for e in ENGS}
        self.dmas = []
        self.sem_uses = [0] * N_DMA_SEMS
        self.sem_last = [None] * N_DMA_SEMS

    def _need(self, ins, tok):
        if tok is None:
            return
        if tok[0] == "e":
            _, se, si = tok
            if se == ins.eng and ins.dma is None:
                return
            if self.known[ins.eng].get(se, -1) >= si:
                return
            self.known[ins.eng][se] = si
            ins.deps.append(tok)
        else:
            if tok[1] in self.known_dma[ins.eng]:
                return
            self.known_dma[ins.eng].add(tok[1])
            ins.deps.append(tok)

    def _track(self, ins, reads, writes):
        eng = ins.eng
        me = ("d", ins.dma) if ins.dma is not None else ("e", eng, ins.idx)
        for r in reads:
            w = self.last_w.get(r)
            if w is not None:
                if w[0] == "e" and w[1] == eng and ins.dma is None:
                    if eng != "pe" and self.known[eng].get(eng, -1) < w[2]:
                        self.known[eng][eng] = w[2]
                        ins.deps.append(w)
                else:
                    self._need(ins, w)
            self.readers.setdefault(r, []).append(me)
        for wr in writes:
            w = self.last_w.get(wr)
            if w is not None and not (w[0] == "e" and w[1] == eng and ins.dma is None):
                self._need(ins, w)
            for rd in self.readers.get(wr, ()):
                if rd[0] == "e" and rd[1] == eng and ins.dma is None:
                    continue
                self._need(ins, rd)
            self.readers[wr] = []
            self.last_w[wr] = me

    def op(self, eng, fn, reads=(), writes=()):
        ins = _Ins(eng, fn)
        ins.idx = len(self.streams[eng])
        self._track(ins, reads, writes)
        self.streams[eng].append(ins)
        return ins

    def dma(self, out, in_, reads=(), writes=(), queue="sp", **kw):
        did = len(self.dmas)
        slot = did % N_DMA_SEMS
        self.sem_uses[slot] += 1
        cnt = self.sem_uses[slot]
        prev = self.sem_last[slot]
        self.sem_last[slot] = did
        self.dmas.append((slot, cnt))
        ins = _Ins(queue, lambda e: e.dma_start(out=out, in_=in_, **kw), dma=did)
        ins.idx = len(self.streams[queue])
        if prev is not None:
            self._need(ins, ("d", prev))
        self._track(ins, reads, writes)
        self.streams[queue].append(ins)
        return ins

    def emit(self, final_wait_all=True):
        nc = self.nc
        for e in ENGS:
            for ins in self.streams[e]:
                for d in ins.deps:
                    if d[0] == "e":
                        self.streams[d[1]][d[2]].inc = True
        vals = {}
        for e in ENGS:
            c = 0
            for ins in self.streams[e]:
                if ins.dma is None and ins.inc:
                    c += 1
                    vals[(e, ins.idx)] = c
        with contextlib.ExitStack() as st:
            esem = {e: st.enter_context(nc.semaphore("s_" + e)) for e in ENGS}
            dsem = [st.enter_context(nc.semaphore("d_%d" % i)) for i in range(N_DMA_SEMS)]
            block = st.enter_context(nc.Block())
            dmas = self.dmas
            sem_uses = self.sem_uses

            def run(e, engine):
                for ins in self.streams[e]:
                    for d in ins.deps:
                        if d[0] == "e":
                            engine.wait_ge(esem[d[1]], vals[(d[1], d[2])])
                        else:
                            slot, cnt = dmas[d[1]]
                            engine.wait_ge(dsem[slot], 16 * cnt)
                    r = ins.fn(engine)
                    if ins.dma is not None:
                        slot, cnt = dmas[ins.dma]
                        r.then_inc(dsem[slot], 16)
                    elif ins.inc:
                        r.then_inc(esem[e], 1)
                if e == "sp" and final_wait_all:
                    for slot in range(N_DMA_SEMS):
                        if sem_uses[slot]:
                            engine.wait_ge(dsem[slot], 16 * sem_uses[slot])

            @block.sync
            def _(eng):
                run("sp", eng)

            @block.tensor
            def _(eng):
                run("pe", eng)

            @block.scalar
            def _(eng):
                run("act", eng)

            @block.vector
            def _(eng):
                run("dve", eng)

            @block.gpsimd
            def _(eng):
                run("pool", eng)


DN_ALPHA = 4 ** 0.25
LN_EPS = 1e-5
NEG = -1.0e30


def build_post(KD, NT):
    nc = bass.Bass("TRN2", target_bir_lowering=False)
    KC = KD // 128
    T = NT * 128
    mT = nc.dram_tensor("mT", [KC, 128, T], BF16, kind="ExternalInput").ap()
    wo = nc.dram_tensor("wo", [KC, 128, 2048], BF16, kind="ExternalInput").ap()
    xres = nc.dram_tensor("xres", [T, 2048], F32, kind="ExternalInput").ap()
    lnp_d = nc.dram_tensor("lnp", [4, 2048], F32, kind="ExternalInput").ap()
    wq = nc.dram_tensor("wq", [16, 128, 2048], BF16, kind="ExternalInput").ap()
    k12_d = nc.dram_tensor("k12T", [128, 2, 128], F32, kind="ExternalInput").ap()
    idf_d = nc.dram_tensor("identf", [128, 128], F32, kind="ExternalInput").ap()
    idb_d = nc.dram_tensor("identb", [128, 128], BF16, kind="ExternalInput").ap()
    uT = nc.dram_tensor("uT", [32, 128, 8192], BF16, kind="ExternalInput").ap()
    vv = nc.dram_tensor("vv", [32, 128, 8192], BF16, kind="ExternalInput").ap()
    out = nc.dram_tensor("out", [T, 2048], F32, kind="ExternalOutput").ap()
    outb = nc.dram_tensor("outb", [T, 2048], BF16, kind="ExternalOutput").ap()

    with contextlib.ExitStack() as st:
        def sb(name, shape, dt):
            return st.enter_context(nc.sbuf_tensor(name, shape, dt))

        def ps(name, shape, dt):
            return st.enter_context(nc.psum_tensor(name, shape, dt))

        NTB = 2
        assert NT % NTB == 0
        GE = 16
        idf = sb("idf", [128, 128], F32)
        idb = sb("idb", [128, 128], BF16)
        k12f = sb("k12f", [128, 2, 128], F32)
        k12b = sb("k12b", [128, 2, 128], BF16)
        mt = sb("mt", [128, KC, 128], BF16)
        xr = sb("xr", [128, 2048], F32)
        qsb = xr
        x1 = [sb("x1_%d" % i, [128, 2048], F32) for i in range(NTB)]
        x1T = [sb("x1T_%d" % i, [128, 16, 128], BF16) for i in range(NTB)]
        s1 = [sb("s1_%d" % i, [128, 8, 128], F32) for i in range(NTB)]
        s2 = [sb("s2_%d" % i, [128, 8, 128], F32) for i in range(NTB)]
        nlnz = [sb("nlnz_%d" % i, [128, 8], F32) for i in range(NTB)]
        accs = [sb("acc_%d" % i, [128, 2048], F32) for i in range(NTB)]
        Gr = [[sb("G%d_%d" % (i, j), [128, GE, 128], BF16) for j in range(2)] for i in range(NTB)]
        qT = sb("qT", [128, 16, 128], BF16)
        m1 = sb("m1", [128, 8, 16], F32)
        m2 = sb("m2", [128, 8, 16], F32)
        tmp = sb("tmp", [128, 256], F32)
        cand = sb("cand", [128, 16, 16], F32)
        sc16 = sb("sc16", [128, 8, 16], F32)
        ex = sb("ex", [128, 8, 16], F32)
        zz = sb("zz", [128, 8], F32)
        st6 = sb("st6", [128, 4, 6], F32)
        mv = sb("mv", [128, 2], F32)
        rstd = sb("rstd", [128, 1], F32)
        Db = [sb("D%d" % i, [128, GE, 128], F32) for i in range(2)]
        lnb = Db[0][:].rearrange("p a b -> p (a b)")
        Eb = [sb("E%d" % i, [128, GE, 128], BF16) for i in range(2)]
        Gh1 = sb("Gh0", [128, GE, 128], BF16)
        Gh = [Gh1, Gh1]
        NUB = 2
        NCB = 2
        NVB = 2
        ub = [sb("ub%d" % i, [128, 16, 512], BF16) for i in range(NUB)]
        vb = [sb("vb%d" % i, [128, 4, 2048], BF16) for i in range(NVB)]
        cb = [sb("cb%d" % i, [128, 2048], BF16) for i in range(NCB)]
        gel = [sb("gel%d" % i, [128, 512], BF16) for i in range(NTB)]
        wsb = [sb("wsb%d" % i, [128, 512], BF16) for i in range(NTB)]
        wt = [sb("wt%d" % i, [128, 4, 128], BF16) for i in range(NTB)]

        acc4 = ps("acc4", [128, 2048], F32)
        pab = [ps("pa", [128, 512], F32), ps("pb", [128, 512], F32)]
        pg = [ps("pg0", [128, 512], BF16), ps("pg1", [128, 512], BF16)]

        P = Prog(nc)
        cnt = {"cb": 0, "ub": 0, "vb": 0, "pab": 0, "d": 0}

        P.dma(idf[:], idf_d, writes=["idf"])
        P.dma(idb[:], idb_d, writes=["idb"])
        P.dma(k12f[:], k12_d, writes=["k12f"])
        P.op("dve", lambda e: e.tensor_copy(out=k12b[:], in_=k12f[:]), reads=["k12f"], writes=["k12b"])

        def next_pab():
            i = cnt["pab"] % 2
            cnt["pab"] += 1
            return i

        def layer_norm(buf, res, gi, bi, dst, dst_res):
            for q in range(4):
                P.op("dve", lambda e, q=q: e.bn_stats(out=st6[:, q, :], in_=buf[:, q * 512:(q + 1) * 512]),
                     reads=[res], writes=[("st6", q)])
            P.op("dve", lambda e: e.bn_aggr(out=mv[:], in_=st6[:].rearrange("p a b -> p (a b)")),
                 reads=[("st6", q) for q in range(4)], writes=["mv"])
            P.op("act", lambda e: e.activation(out=rstd[:], in_=mv[:, 1:2], func=AF.Sqrt, bias=LN_EPS, scale=1.0),
                 reads=["mv"], writes=["rstd"])
            P.op("dve", lambda e: e.reciprocal(out=rstd[:], in_=rstd[:]), reads=["rstd"], writes=["rstd"])
            P.op("dve", lambda e: e.tensor_scalar(out=buf[:], in0=buf[:], scalar1=mv[:, 0:1], scalar2=rstd[:, 0:1],
                                                  op0=ALU.subtract, op1=ALU.mult),
                 reads=[res, "mv", "rstd"], writes=[res])
            P.dma(lnb, lnp_d[gi:gi + 1, :].to_broadcast([128, 2048]), writes=[("D", 0)])
            P.op("dve", lambda e: e.tensor_tensor(out=buf[:], in0=buf[:], in1=lnb, op=ALU.mult),
                 reads=[res, ("D", 0)], writes=[res])
            P.dma(lnb, lnp_d[bi:bi + 1, :].to_broadcast([128, 2048]), writes=[("D", 0)])
            P.op("dve", lambda e: e.tensor_tensor(out=dst, in0=buf[:], in1=lnb, op=ALU.add),
                 reads=[res, ("D", 0)], writes=[dst_res])

        def front(ti, b):
            tsl = slice(ti * 128, (ti + 1) * 128)
            X1, X1T, S1, S2, NL = x1[b], x1T[b], s1[b], s2[b], nlnz[b]
            rx1, rx1T, rs1, rs2, rnl = ("x1", b), ("x1T", b), ("s1", b), ("s2", b), ("nlnz", b)
            P.dma(mt[:], mT[:, :, tsl].rearrange("k p t -> p k t"), writes=["mt"])
            P.dma(xr[:], xres[tsl, :], writes=["xr"])
            for kc in range(KC):
                bb = cnt["cb"] % NCB
                cnt["cb"] += 1
                P.dma(cb[bb][:], wo[kc], writes=[("cb", bb)])
                for dq in range(4):
                    P.op("pe", lambda e, kc=kc, dq=dq, bb=bb: e.matmul(
                        acc4[:, dq * 512:(dq + 1) * 512], lhsT=mt[:, kc, :], rhs=cb[bb][:, dq * 512:(dq + 1) * 512],
                        start=(kc == 0), stop=(kc == KC - 1)),
                        reads=["mt", ("cb", bb)], writes=["acc4"])
            P.op("dve", lambda e: e.scalar_tensor_tensor(out=X1[:], in0=xr[:], scalar=DN_ALPHA, in1=acc4[:],
                                                         op0=ALU.mult, op1=ALU.add),
                 reads=["xr", "acc4"], writes=[rx1])
            layer_norm(X1, rx1, 0, 1, X1[:], rx1)
            for g4 in range(4):
                pi = next_pab()
                for k in range(4):
                    dc = g4 * 4 + k
                    P.op("pe", lambda e, dc=dc, k=k, pi=pi: e.transpose(
                        out=pab[pi][:, k * 128:(k + 1) * 128], in_=X1[:, dc * 128:(dc + 1) * 128], identity=idf[:]),
                        reads=[rx1, "idf"], writes=[("pab", pi)])
                P.op("act", lambda e, g4=g4, pi=pi: e.activation(
                    out=X1T[:, g4 * 4:(g4 + 1) * 4, :], in_=pab[pi][:].rearrange("p (a b) -> p a b", a=4), func=AF.Copy),
                    reads=[("pab", pi)], writes=[rx1T])
            for dc in range(16):
                bb = cnt["cb"] % NCB
                cnt["cb"] += 1
                P.dma(cb[bb][:], wq[dc], writes=[("cb", bb)])
                for dq in range(4):
                    P.op("pe", lambda e, dc=dc, dq=dq, bb=bb: e.matmul(
                        acc4[:, dq * 512:(dq + 1) * 512], lhsT=X1T[:, dc, :], rhs=cb[bb][:, dq * 512:(dq + 1) * 512],
                        start=(dc == 0), stop=(dc == 15)),
                        reads=[rx1T, ("cb", bb)], writes=["acc4"])
            P.op("act", lambda e: e.activation(out=qsb[:], in_=acc4[:], func=AF.Copy), reads=["acc4"], writes=["xr"])
            for g4 in range(4):
                pi = next_pab()
                for k in range(4):
                    j = g4 * 4 + k
                    P.op("pe", lambda e, j=j, k=k, pi=pi: e.transpose(
                        out=pab[pi][:, k * 128:(k + 1) * 128], in_=qsb[:, j * 128:(j + 1) * 128], identity=idf[:]),
                        reads=["xr", "idf"], writes=[("pab", pi)])
                P.op("act", lambda e, g4=g4, pi=pi: e.activation(
                    out=qT[:, g4 * 4:(g4 + 1) * 4, :], in_=pab[pi][:].rearrange("p (a b) -> p a b", a=4), func=AF.Copy),
                    reads=[("pab", pi)], writes=["qT"])
            for half, sdst, sres in ((0, S1, rs1), (1, S2, rs2)):
                for g2 in range(2):
                    pi = next_pab()
                    for k in range(4):
                        h = g2 * 4 + k
                        P.op("pe", lambda e, h=h, k=k, pi=pi, half=half: e.matmul(
                            pab[pi][:, k * 128:(k + 1) * 128], lhsT=qT[:, 2 * h + half, :], rhs=k12b[:, half, :],
                            start=True, stop=True),
                            reads=["qT", "k12b"], writes=[("pab", pi)])
                    P.op("act", lambda e, g2=g2, pi=pi, sdst=sdst: e.activation(
                        out=sdst[:, g2 * 4:(g2 + 1) * 4, :], in_=pab[pi][:].rearrange("p (a b) -> p a b", a=4), func=AF.Copy),
                        reads=[("pab", pi)], writes=[sres])
            for sdst, sres, mm, mres in ((S1, rs1, m1, "m1"), (S2, rs2, m2, "m2")):
                for h in range(8):
                    P.op("dve", lambda e, h=h, sdst=sdst, mm=mm: e.max(out=mm[:, h, 0:8], in_=sdst[:, h, :]),
                         reads=[sres], writes=[mres])
                    P.op("dve", lambda e, h=h, sdst=sdst, mm=mm: e.match_replace(
                        out=tmp[:, 0:128], in_to_replace=mm[:, h, 0:8], in_values=sdst[:, h, :], imm_value=NEG),
                        reads=[sres, mres], writes=["tmp"])
                    P.op("dve", lambda e, h=h, mm=mm: e.max(out=mm[:, h, 8:16], in_=tmp[:, 0:128]),
                         reads=["tmp"], writes=[mres])
            for h in range(8):
                P.op("dve", lambda e, h=h: e.tensor_tensor(
                    out=cand[:], in0=m1[:, h, :].unsqueeze(2).to_broadcast([128, 16, 16]),
                    in1=m2[:, h, :].unsqueeze(1).to_broadcast([128, 16, 16]), op=ALU.add),
                    reads=["m1", "m2"], writes=["cand"])
                P.op("dve", lambda e, h=h: e.max(out=sc16[:, h, 0:8], in_=cand[:].rearrange("p a b -> p (a b)")),
                     reads=["cand"], writes=["sc16"])
                P.op("dve", lambda e, h=h: e.match_replace(
                    out=tmp[:], in_to_replace=sc16[:, h, 0:8], in_values=cand[:].rearrange("p a b -> p (a b)"),
                    imm_value=NEG), reads=["cand", "sc16"], writes=["tmp"])
                P.op("dve", lambda e, h=h: e.max(out=sc16[:, h, 8:16], in_=tmp[:]),
                     reads=["tmp"], writes=["sc16"])
                P.op("dve", lambda e, h=h: e.tensor_scalar(out=ex[:, h, :], in0=sc16[:, h, :], scalar1=sc16[:, h, 15:16],
                                                           scalar2=None, op0=ALU.subtract),
                     reads=["sc16"], writes=["ex"])
                P.op("dve", lambda e, h=h: e.tensor_scalar(out=S1[:, h, :], in0=S1[:, h, :], scalar1=sc16[:, h, 15:16],
                                                           scalar2=None, op0=ALU.subtract),
                     reads=["sc16", rs1], writes=[rs1])
            P.op("act", lambda e: e.activation(out=ex[:], in_=ex[:], func=AF.Exp), reads=["ex"], writes=["ex"])
            P.op("dve", lambda e: e.tensor_reduce(out=zz[:], in_=ex[:], axis=AX.X, op=ALU.add),
                 reads=["ex"], writes=["zz"])
            P.op("act", lambda e: e.activation(out=zz[:], in_=zz[:], func=AF.Ln), reads=["zz"], writes=["zz"])
            P.op("dve", lambda e: e.tensor_scalar(out=NL[:], in0=zz[:], scalar1=-1.0, scalar2=None, op0=ALU.mult),
                 reads=["zz"], writes=[rnl])

        def ggen(h, gq, b):
            di = cnt["d"] % 2
            cnt["d"] += 1
            esl = slice(gq * GE, (gq + 1) * GE)
            Gd = Gr[b][gq % 2]
            rg = ("G", b, gq % 2)
            P.op("dve", lambda e: e.tensor_tensor(
                out=Db[di][:], in0=s1[b][:, h, esl].unsqueeze(2).to_broadcast([128, GE, 128]),
                in1=s2[b][:, h, :].unsqueeze(1).to_broadcast([128, GE, 128]), op=ALU.add),
                reads=[("s1", b), ("s2", b)], writes=[("D", di)])
            P.op("act", lambda e: e.activation(
                out=Eb[di][:], in_=Db[di][:], func=AF.Exp, bias=nlnz[b][:, h:h + 1], scale=1.0),
                reads=[("D", di), ("nlnz", b)], writes=[("E", di)])
            if h == 0:
                P.op("dve", lambda e: e.scalar_tensor_tensor(
                    out=Gd[:], in0=Db[di][:], scalar=0.0, in1=Eb[di][:], op0=ALU.is_ge, op1=ALU.mult),
                    reads=[("D", di), ("E", di)], writes=[rg])
            else:
                P.op("dve", lambda e: e.scalar_tensor_tensor(
                    out=Gh[di][:], in0=Db[di][:], scalar=0.0, in1=Eb[di][:], op0=ALU.is_ge, op1=ALU.mult),
                    reads=[("D", di), ("E", di)], writes=["Gh"])
                P.op("dve", lambda e: e.tensor_tensor(out=Gd[:], in0=Gd[:], in1=Gh[di][:], op=ALU.add),
                     reads=["Gh", rg], writes=[rg])

        EPG = GE // 4
        NGQ = 128 // GE

        def stage_a(eg, b, u_i):
            gq, k = divmod(eg, EPG)
            for dc in range(16):
                P.op("pe", lambda e, dc=dc: e.matmul(
                    pab[b][:], lhsT=x1T[b][:, dc, :], rhs=ub[u_i][:, dc, :], start=(dc == 0), stop=(dc == 15)),
                    reads=[("ub", u_i), ("x1T", b)], writes=[("pab", b)])
            P.op("act", lambda e: e.activation(out=gel[b][:], in_=pab[b][:], func=AF.Gelu),
                 reads=[("pab", b)], writes=[("gel", b)])
            P.op("dve", lambda e: e.tensor_tensor(
                out=wsb[b][:], in0=gel[b][:], in1=Gr[b][gq % 2][:, k * 4:(k + 1) * 4, :].rearrange("p a b -> p (a b)"), op=ALU.mult),
                reads=[("gel", b), ("G", b, gq % 2)], writes=[("wsb", b)])

        def stage_b(eg, b, v_i):
            for j in range(4):
                P.op("pe", lambda e, j=j: e.transpose(
                    out=pg[b][:, j * 128:(j + 1) * 128], in_=wsb[b][:, j * 128:(j + 1) * 128], identity=idb[:]),
                    reads=[("wsb", b), "idb"], writes=[("pg", b)])
            P.op("act", lambda e: e.activation(
                out=wt[b][:], in_=pg[b][:].rearrange("p (a b) -> p a b", a=4), func=AF.Copy),
                reads=[("pg", b)], writes=[("wt", b)])
            for j in range(4):
                for dq in range(4):
                    P.op("pe", lambda e, j=j, dq=dq: e.matmul(
                        acc4[:, dq * 512:(dq + 1) * 512], lhsT=wt[b][:, j, :], rhs=vb[v_i][:, j, dq * 512:(dq + 1) * 512],
                        start=(j == 0), stop=(j == 3)),
                        reads=[("wt", b), ("vb", v_i)], writes=["acc4"])
            if eg == 0:
                P.op("act", lambda e: e.activation(out=accs[b][:], in_=acc4[:], func=AF.Copy),
                     reads=["acc4"], writes=[("acc", b)])
            else:
                P.op("dve", lambda e: e.tensor_tensor(out=accs[b][:], in0=accs[b][:], in1=acc4[:], op=ALU.add),
                     reads=["acc4", ("acc", b)], writes=[("acc", b)])

        for blk in range(NT // NTB):
            for b in range(NTB):
                front(blk * NTB + b, b)
            for b in range(NTB):
                for h in range(8):
                    ggen(h, 0, b)
            for gq in range(NGQ):
                for k in range(EPG):
                    eg = gq * EPG + k
                    u_i = cnt["ub"] % NUB
                    cnt["ub"] += 1
                    v_i = cnt["vb"] % NVB
                    cnt["vb"] += 1
                    P.dma(ub[u_i][:].rearrange("p a b -> p (a b)"), uT[eg], writes=[("ub", u_i)])
                    P.dma(vb[v_i][:].rearrange("p a b -> p (a b)"), vv[eg], writes=[("vb", v_i)])
                    for b in range(NTB):
                        stage_a(eg, b, u_i)
                    for b in range(NTB):
                        stage_b(eg, b, v_i)
                    if gq + 1 < NGQ:
                        for b in range(NTB):
                            for h in range(k * 8 // EPG, (k + 1) * 8 // EPG):
                                ggen(h, gq + 1, b)
            for b in range(NTB):
                ti = blk * NTB + b
                tsl = slice(ti * 128, (ti + 1) * 128)
                P.op("dve", lambda e, b=b: e.scalar_tensor_tensor(out=x1[b][:], in0=x1[b][:], scalar=DN_ALPHA, in1=accs[b][:],
                                                                  op0=ALU.mult, op1=ALU.add),
                     reads=[("x1", b), ("acc", b)], writes=[("x1", b)])
                layer_norm(x1[b], ("x1", b), 2, 3, xr[:], "xr")
                P.dma(out[tsl, :], xr[:], reads=["xr"], queue="pool")
                P.op("act", lambda e: e.activation(out=cb[0][:], in_=xr[:], func=AF.Copy), reads=["xr"], writes=[("cb", 0)])
                P.dma(outb[tsl, :], cb[0][:], reads=[("cb", 0)], queue="pool")
        P.emit()
    return nc


NEG = -1.0e30


def build_fox(S, NH):
    nc = bass.Bass("TRN2", target_bir_lowering=False)
    NB = S // 512
    NQ = S // 128
    NG = S // 512
    xT = nc.dram_tensor("xT", [16, 128, S], BF16, kind="ExternalInput").ap()
    wqk_d = nc.dram_tensor("wqk", [NH, 128, 16, 256], BF16, kind="ExternalInput").ap()
    wv_d = nc.dram_tensor("wv", [128, 16, NH * 128], BF16, kind="ExternalInput").ap()
    wf_d = nc.dram_tensor("wf", [128, 16, NH], BF16, kind="ExternalInput").ap()
    bf_d = nc.dram_tensor("bf", [NH, 1], F32, kind="ExternalInput").ap()
    mk_d = nc.dram_tensor("maskT", [128, 128], BF16, kind="ExternalInput").ap()
    idb_d = nc.dram_tensor("identb", [128, 128], BF16, kind="ExternalInput").ap()
    idf_d = nc.dram_tensor("identf", [128, 128], F32, kind="ExternalInput").ap()
    oT = nc.dram_tensor("oT", [NH, 128, S], BF16, kind="ExternalOutput").ap()
    negc_d = nc.dram_tensor("negc_scr", [NH, S], F32).ap()
    bsc_d = nc.dram_tensor("b_scr", [NH, 1], F32).ap()

    with contextlib.ExitStack() as st:
        def sb(name, shape, dt):
            return st.enter_context(nc.sbuf_tensor(name, shape, dt))

        def ps(name, shape, dt):
            return st.enter_context(nc.psum_tensor(name, shape, dt))

        xb = [sb("xb%d" % i, [128, 16, 512], BF16) for i in range(2)]
        w = sb("w", [128, 16, 256], BF16)
        wv = sb("wv_s", [128, 16, NH * 128], BF16)
        wf = sb("wf_s", [128, 16, NH], BF16)
        bfs = sb("bfs", [NH, 1], F32)
        nbf = sb("nbf", [NH, 1], F32)
        maskT = sb("maskT_s", [128, 128], BF16)
        idb = sb("idb", [128, 128], BF16)
        idf = sb("idf", [128, 128], F32)
        ones_b = sb("ones_b", [128, 1], BF16)
        ones_f = sb("ones_f", [1, 128], F32)
        qT = sb("qT", [128, S], BF16)
        kT = sb("kT", [128, S], BF16)
        v = sb("v", [128, NQ, NH * 128], BF16)
        lsp = sb("lsp", [NH, 512], F32)
        ngb = [sb("ngb%d" % i, [NH, 512], F32) for i in range(2)]
        sq = sb("sq", [128, 512], BF16)
        qmx = sb("qmx", [1, NB], F32)
        kmx = sb("kmx", [1, NB], F32)
        q2 = sb("q2", [1, 1], F32)
        k2 = sb("k2", [1, 1], F32)
        Bb = sb("Bb", [128, 1], F32)
        refb = sb("refb", [128, NG], F32)
        ncr = sb("ncr", [NQ, 128], F32)
        negc_col = sb("negc_col", [128, NQ], F32)
        biasmat = sb("biasmat", [128, NG, NQ], F32)
        pT = [sb("pT%d" % i, [128, 512], BF16) for i in range(2)]
        rrow = sb("rrow", [1, 512], F32)
        osb = sb("osb", [128, 512], F32)
        oTs = sb("oTs", [128, S], BF16)

        pq = [ps("pq%d" % i, [128, 512], F32) for i in range(4)]
        po = ps("po", [128, 512], F32)
        psm = ps("psm", [128, 512], F32)
        pbc = ps("pbc", [128, 512], F32)
        pn = ps("pn", [128, 512], F32)

        P = Prog(nc)
        cnt = {"pq": 0, "x": 0, "pt": 0}

        def nb():
            i = cnt["pq"] % 4
            cnt["pq"] += 1
            return i

        P.dma(wv[:], wv_d, writes=["wv"])
        P.dma(wf[:], wf_d, writes=["wf"])
        P.dma(bfs[:], bf_d, writes=["bfs"])
        P.dma(maskT[:], mk_d, writes=["maskT"])
        P.dma(idb[:], idb_d, writes=["idb"])
        P.dma(idf[:], idf_d, writes=["idf"])
        P.op("dve", lambda e: e.tensor_scalar(out=nbf[:], in0=bfs[:], scalar1=-1.0, scalar2=None, op0=ALU.mult),
             reads=["bfs"], writes=["nbf"])
        P.op("pool", lambda e: e.memset(ones_b[:], 1.0), writes=["ones_b"])
        P.op("pool", lambda e: e.memset(ones_f[:], 1.0), writes=["ones_f"])

        def load_x(blk):
            xi = cnt["x"] % 2
            cnt["x"] += 1
            P.dma(xb[xi][:], xT[:, :, blk * 512:(blk + 1) * 512].rearrange("k p t -> p k t"), writes=[("xb", xi)])
            return xi

        for blk in range(NB):
            xi = load_x(blk)
            bsl = slice(blk * 512, (blk + 1) * 512)
            for tt in range(4):
                pi = nb()
                for dc in range(16):
                    P.op("pe", lambda e, dc=dc, pi=pi, xi=xi, tt=tt: e.matmul(
                        pq[pi][:, 0:NH * 128], lhsT=xb[xi][:, dc, tt * 128:(tt + 1) * 128], rhs=wv[:, dc, :],
                        start=(dc == 0), stop=(dc == 15)),
                        reads=["wv", ("xb", xi)], writes=[("pq", pi)])
                eng = "act" if tt % 2 == 0 else "dve"
                if eng == "act":
                    P.op("act", lambda e, pi=pi, blk=blk, tt=tt: e.activation(
                        out=v[:, blk * 4 + tt, :], in_=pq[pi][:, 0:NH * 128], func=AF.Copy),
                        reads=[("pq", pi)], writes=["v"])
                else:
                    P.op("dve", lambda e, pi=pi, blk=blk, tt=tt: e.tensor_copy(
                        out=v[:, blk * 4 + tt, :], in_=pq[pi][:, 0:NH * 128]),
                        reads=[("pq", pi)], writes=["v"])
            for dc in range(16):
                P.op("pe", lambda e, dc=dc, xi=xi: e.matmul(
                    pn[0:NH, :], lhsT=wf[:, dc, :], rhs=xb[xi][:, dc, :], start=(dc == 0), stop=(dc == 15)),
                    reads=["wf", ("xb", xi)], writes=["pn"])
            P.op("act", lambda e: e.activation(out=lsp[:], in_=pn[0:NH, :], func=AF.Exp, bias=nbf[:, 0:1], scale=-1.0),
                 reads=["pn", "nbf"], writes=["lsp"])
            P.op("act", lambda e: e.activation(out=lsp[:], in_=lsp[:], func=AF.Ln, bias=1.0, scale=1.0),
                 reads=["lsp"], writes=["lsp"])
            gi = blk % 2
            if blk == 0:
                P.op("dve", lambda e, gi=gi: e.tensor_tensor_scan(
                    out=ngb[gi][:], data0=lsp[:], data1=lsp[:], initial=0.0, op0=ALU.add, op1=ALU.max),
                    reads=["lsp"], writes=[("ngb", gi)])
            else:
                P.op("dve", lambda e, gi=gi: e.tensor_tensor_scan(
                    out=ngb[gi][:], data0=lsp[:], data1=lsp[:], initial=ngb[1 - gi][:, 511:512], op0=ALU.add, op1=ALU.max),
                    reads=["lsp", ("ngb", 1 - gi)], writes=[("ngb", gi)])
            P.dma(negc_d[:, bsl], ngb[gi][:], reads=[("ngb", gi)], writes=["negc_d"], queue="pool")

        for h in range(NH):
            P.dma(w[:], wqk_d[h], writes=["w"])
            for blk in range(NB):
                xi = load_x(blk)
                bsl = slice(blk * 512, (blk + 1) * 512)
                for which, dst, dres, scale, mxt, mres in ((0, qT, "qT", 128 ** -0.5, qmx, "qmx"), (1, kT, "kT", 1.0, kmx, "kmx")):
                    pi = nb()
                    for dc in range(16):
                        P.op("pe", lambda e, dc=dc, pi=pi, xi=xi, which=which: e.matmul(
                            pq[pi][:], lhsT=w[:, dc, which * 128:(which + 1) * 128], rhs=xb[xi][:, dc, :],
                            start=(dc == 0), stop=(dc == 15)),
                            reads=["w", ("xb", xi)], writes=[("pq", pi)])
                    P.op("act", lambda e, pi=pi, dst=dst, bsl=bsl, scale=scale: e.activation(
                        out=dst[:, bsl], in_=pq[pi][:], func=AF.Copy, scale=scale),
                        reads=[("pq", pi)], writes=[dres])
                    P.op("dve", lambda e, dst=dst, bsl=bsl: e.tensor_tensor(out=sq[:], in0=dst[:, bsl], in1=dst[:, bsl], op=ALU.mult),
                         reads=[dres], writes=["sq"])
                    P.op("pe", lambda e: e.matmul(pn[0:1, :], lhsT=ones_b[:], rhs=sq[:], start=True, stop=True),
                         reads=["ones_b", "sq"], writes=["pn"])
                    P.op("dve", lambda e, mxt=mxt, blk=blk: e.tensor_reduce(
                        out=mxt[:, blk:blk + 1], in_=pn[0:1, :], axis=AX.X, op=ALU.max),
                        reads=["pn"], writes=[mres])
            P.op("dve", lambda e: e.tensor_reduce(out=q2[:], in_=qmx[:], axis=AX.X, op=ALU.max), reads=["qmx"], writes=["q2"])
            P.op("dve", lambda e: e.tensor_reduce(out=k2[:], in_=kmx[:], axis=AX.X, op=ALU.max), reads=["kmx"], writes=["k2"])
            P.op("dve", lambda e: e.tensor_tensor(out=q2[:], in0=q2[:], in1=k2[:], op=ALU.mult), reads=["q2", "k2"], writes=["q2"])
            P.op("act", lambda e: e.activation(out=q2[:], in_=q2[:], func=AF.Sqrt, scale=1.1025), reads=["q2"], writes=["q2"])
            P.dma(bsc_d[h:h + 1, :], q2[:], reads=["q2"], writes=[("bsc", h)], queue="pool")
            P.dma(Bb[:], bsc_d[h:h + 1, :].to_broadcast([128, 1]), reads=[("bsc", h)], writes=["Bb"])
            P.dma(refb[:], negc_d[h:h + 1, :].rearrange("o (g c) -> o g c", c=512)[:, :, 511].to_broadcast([128, NG]),
                  reads=["negc_d"], writes=["refb"], allow_slow_non_contiguous=True)
            P.op("dve", lambda e: e.tensor_scalar(out=refb[:], in0=refb[:], scalar1=Bb[:, 0:1], scalar2=None, op0=ALU.add),
                 reads=["refb", "Bb"], writes=["refb"])
            P.dma(ncr[:], negc_d[h:h + 1, :].rearrange("o (k p) -> (o k) p", p=128), reads=["negc_d"], writes=["ncr"])
            P.op("pe", lambda e: e.transpose(out=pn[:, 0:NQ], in_=ncr[:], identity=idf[0:NQ, 0:NQ]),
                 reads=["ncr", "idf"], writes=["pn"])
            P.op("dve", lambda e: e.tensor_copy(out=negc_col[:], in_=pn[:, 0:NQ]), reads=["pn"], writes=["negc_col"])
            for qg in range(NG):
                P.op("dve", lambda e, qg=qg: e.tensor_scalar(
                    out=biasmat[:, qg, :], in0=negc_col[:], scalar1=refb[:, qg:qg + 1], scalar2=None, op0=ALU.subtract),
                    reads=["negc_col", "refb"], writes=["biasmat"])
            def pv_stage(ti, kb, off, N, nkb, h):
                P.op("pe", lambda e: e.matmul(
                    po[:, off:512], lhsT=v[:, kb, h * 128:(h + 1) * 128], rhs=pT[ti][:, 0:N],
                    start=(kb == 0), stop=(kb == nkb - 1)),
                    reads=["v", ("pT", ti)], writes=["po"])
                P.op("pe", lambda e: e.matmul(
                    psm[0:1, off:512], lhsT=ones_b[:], rhs=pT[ti][:, 0:N], start=(kb == 0), stop=(kb == nkb - 1)),
                    reads=["ones_b", ("pT", ti)], writes=["psm"])

            for qg in range(NG):
                q0 = qg * 512
                nkb = 4 * qg + 4
                pend = None
                for kb in range(nkb):
                    off = max(0, kb * 128 - q0)
                    N = 512 - off
                    diag = kb >= 4 * qg
                    pi = nb()
                    ti = cnt["pt"] % 2
                    cnt["pt"] += 1
                    P.op("pe", lambda e, pi=pi, kb=kb, q0=q0, off=off, N=N, diag=diag: e.matmul(
                        pq[pi][:, 0:N], lhsT=kT[:, kb * 128:(kb + 1) * 128], rhs=qT[:, q0 + off:q0 + 512],
                        start=True, stop=(not diag)),
                        reads=["qT", "kT"], writes=[("pq", pi)])
                    if diag:
                        P.op("pe", lambda e, pi=pi: e.matmul(
                            pq[pi][:, 0:128], lhsT=idb[:], rhs=maskT[:], start=False, stop=True),
                            reads=["idb", "maskT"], writes=[("pq", pi)])
                    P.op("act", lambda e, pi=pi, ti=ti, N=N, qg=qg, kb=kb: e.activation(
                        out=pT[ti][:, 0:N], in_=pq[pi][:, 0:N], func=AF.Exp, bias=biasmat[:, qg, kb:kb + 1], scale=1.0),
                        reads=[("pq", pi), "biasmat"], writes=[("pT", ti)])
                    if pend is not None:
                        pv_stage(*pend)
                    pend = (ti, kb, off, N, nkb, h)
                pv_stage(*pend)
                P.op("dve", lambda e: e.reciprocal(out=rrow[:], in_=psm[0:1, :]), reads=["psm"], writes=["rrow"])
                P.op("pe", lambda e: e.matmul(pbc[:], lhsT=ones_f[:], rhs=rrow[:], start=True, stop=True),
                     reads=["ones_f", "rrow"], writes=["pbc"])
                P.op("act", lambda e: e.activation(out=osb[:], in_=po[:], func=AF.Copy), reads=["po"], writes=["osb"])
                P.op("dve", lambda e, q0=q0: e.tensor_tensor(out=oTs[:, q0:q0 + 512], in0=osb[:], in1=pbc[:], op=ALU.mult),
                     reads=["osb", "pbc"], writes=["oTs"])
            P.dma(oT[h], oTs[:], reads=["oTs"], queue="pool")
        P.emit()
    return nc


LN_EPS = 1e-5


def build_ret(S, NH):
    nc = bass.Bass("TRN2", target_bir_lowering=False)
    NB = S // 512
    NQ = S // 128
    NG = S // 512
    xT = nc.dram_tensor("xT", [16, 128, S], BF16, kind="ExternalInput").ap()
    wqk_d = nc.dram_tensor("wqk", [NH, 128, 16, 512], BF16, kind="ExternalInput").ap()
    wv_d = nc.dram_tensor("wv", [NH, 128, 16, 512], BF16, kind="ExternalInput").ap()
    wg_d = nc.dram_tensor("wg", [NH, 128, 16, 512], BF16, kind="ExternalInput").ap()
    cos_d = nc.dram_tensor("cosT", [128, S], F32, kind="ExternalInput").ap()
    sin_d = nc.dram_tensor("sinT", [128, S], F32, kind="ExternalInput").ap()
    tz_d = nc.dram_tensor("tz", [NH, 128, S], BF16, kind="ExternalInput").ap()
    dg_d = nc.dram_tensor("dgm", [NH, 128, 128], BF16, kind="ExternalInput").ap()
    gn_d = nc.dram_tensor("gng", [NH, 512], F32, kind="ExternalInput").ap()
    mo = nc.dram_tensor("mo", [S, NH * 512], BF16, kind="ExternalOutput").ap()

    with contextlib.ExitStack() as st:
        def sb(name, shape, dt):
            return st.enter_context(nc.sbuf_tensor(name, shape, dt))

        def ps(name, shape, dt):
            return st.enter_context(nc.psum_tensor(name, shape, dt))

        xb = sb("xb", [128, 16, 512], BF16)
        wqk = sb("wqk_s", [128, 16, 512], BF16)
        assert S == 8192 or S <= 8192
        tzv = wqk[:].rearrange("p a b -> p (a b)")
        wvg = sb("wvg_s", [128, 16, 512], BF16)
        cs = sb("cs", [128, 512], F32)
        sn = sb("sn", [128, 512], F32)
        tt = [sb("tt%d" % i, [128, 512], F32) for i in range(2)]
        qT = [sb("qT%d" % i, [128, S], BF16) for i in range(2)]
        kT = [sb("kT%d" % i, [128, S], BF16) for i in range(2)]
        v = sb("v", [128, NQ, 512], BF16)
        dgm = sb("dgm_s", [128, 128], BF16)
        gng = sb("gng_s", [128, 512], F32)
        wT = [sb("wT%d" % i, [128, 512], BF16) for i in range(2)]
        st6 = sb("st6", [128, 6], F32)
        mv = sb("mv", [128, 2], F32)
        rstd = sb("rstd", [128, 1], F32)
        on = sb("on", [128, 512], F32)
        sg = sb("sg", [128, 512], F32)
        tt = tt + [on, sg]
        ttres = [("tt", 0), ("tt", 1), "on", "sg"]
        res = [sb("res%d" % i, [128, 512], BF16) for i in range(2)]

        pb = [ps("pb%d" % i, [128, 512], F32) for i in range(8)]

        P = Prog(nc)
        cnt = {"p": 0, "w": 0, "r": 0}

        def nb():
            i = cnt["p"] % 8
            cnt["p"] += 1
            return i

        for h in range(NH):
            P.dma(wqk[:], wqk_d[h], writes=["wqk"])
            P.dma(wvg[:], wv_d[h], writes=["wvg"])
            P.dma(dgm[:], dg_d[h], writes=["dgm"])
            P.dma(gng[:], gn_d[h:h + 1, :].to_broadcast([128, 512]), writes=["gng"])
            for blk in range(NB):
                bsl = slice(blk * 512, (blk + 1) * 512)
                P.dma(xb[:], xT[:, :, bsl].rearrange("k p t -> p k t"), writes=["xb"])
                P.dma(cs[:], cos_d[:, bsl], writes=["cs"])
                P.dma(sn[:], sin_d[:, bsl], writes=["sn"])
                for which, dst, dres, scale in ((0, qT, "qT", 1.0), (1, kT, "kT", 0.0625)):
                    pa = nb()
                    pc = nb()
                    for half, pi in ((0, pa), (1, pc)):
                        c0 = which * 256 + half * 128
                        for dc in range(16):
                            P.op("pe", lambda e, dc=dc, pi=pi, c0=c0: e.matmul(
                                pb[pi][:], lhsT=wqk[:, dc, c0:c0 + 128], rhs=xb[:, dc, :], start=(dc == 0), stop=(dc == 15)),
                                reads=["wqk", "xb"], writes=[("pb", pi)])
                    for ti, (src, tab, tres) in enumerate(((pa, cs, "cs"), (pc, sn, "sn"), (pa, sn, "sn"), (pc, cs, "cs"))):
                        P.op("dve", lambda e, ti=ti, src=src, tab=tab, scale=scale: e.scalar_tensor_tensor(
                            out=tt[ti][:], in0=pb[src][:], scalar=scale, in1=tab[:], op0=ALU.mult, op1=ALU.mult),
                            reads=[("pb", src), tres], writes=[ttres[ti]])
                    P.op("dve", lambda e, dst=dst, bsl=bsl: e.tensor_tensor(out=dst[0][:, bsl], in0=tt[0][:], in1=tt[1][:], op=ALU.subtract),
                         reads=[ttres[0], ttres[1]], writes=[dres + "0"])
                    P.op("dve", lambda e, dst=dst, bsl=bsl: e.tensor_tensor(out=dst[1][:, bsl], in0=tt[2][:], in1=tt[3][:], op=ALU.add),
                         reads=[ttres[2], ttres[3]], writes=[dres + "1"])
                for t4 in range(4):
                    pi = nb()
                    for dc in range(16):
                        P.op("pe", lambda e, dc=dc, pi=pi, t4=t4: e.matmul(
                            pb[pi][:], lhsT=xb[:, dc, t4 * 128:(t4 + 1) * 128], rhs=wvg[:, dc, :], start=(dc == 0), stop=(dc == 15)),
                            reads=["wvg", "xb"], writes=[("pb", pi)])
                    P.op("act", lambda e, pi=pi, blk=blk, t4=t4: e.activation(out=v[:, blk * 4 + t4, :], in_=pb[pi][:], func=AF.Copy),
                         reads=[("pb", pi)], writes=["v"])
            P.dma(tzv[:, 0:S], tz_d[h], writes=["wqk"])
            P.dma(wvg[:], wg_d[h], writes=["wvg"])
            po = [0, 1, 2, 3]

            def pv_stage(wi, kb, off, qg):
                for j in range(off // 128, 4):
                    tb = 4 * qg + j
                    c0 = j * 128 - off
                    P.op("pe", lambda e, j=j, c0=c0, tb=tb: e.matmul(
                        pb[po[j]][:], lhsT=wT[wi][:, c0:c0 + 128], rhs=v[:, kb, :], start=(kb == 0), stop=(kb == tb)),
                        reads=[("wT", wi), "v"], writes=[("pb", po[j])])

            for qg in range(NG):
                q0 = qg * 512
                nkb = 4 * qg + 4
                pend = None
                for kb in range(nkb):
                    off = max(0, kb * 128 - q0)
                    N = 512 - off
                    diag = kb >= 4 * qg
                    pi = 4 + cnt["w"] % 2
                    wi = cnt["w"] % 2
                    cnt["w"] += 1
                    for c in range(2):
                        P.op("pe", lambda e, pi=pi, kb=kb, q0=q0, off=off, N=N, c=c: e.matmul(
                            pb[pi][:, 0:N], lhsT=kT[c][:, kb * 128:(kb + 1) * 128], rhs=qT[c][:, q0 + off:q0 + 512],
                            start=(c == 0), stop=(c == 1)),
                            reads=["qT0", "qT1", "kT0", "kT1"], writes=[("pb", pi)])
                    if diag:
                        P.op("dve", lambda e, pi=pi, wi=wi: e.tensor_tensor(
                            out=wT[wi][:, 0:128], in0=pb[pi][:, 0:128], in1=dgm[:], op=ALU.mult),
                            reads=[("pb", pi), "dgm"], writes=[("wT", wi)])
                        if N > 128:
                            P.op("dve", lambda e, pi=pi, wi=wi, N=N: e.tensor_tensor(
                                out=wT[wi][:, 128:N], in0=pb[pi][:, 128:N], in1=tzv[:, 128:N], op=ALU.mult),
                                reads=[("pb", pi), "wqk"], writes=[("wT", wi)])
                    else:
                        n0 = q0 - kb * 128
                        P.op("dve", lambda e, pi=pi, wi=wi, n0=n0: e.tensor_tensor(
                            out=wT[wi][:], in0=pb[pi][:], in1=tzv[:, n0:n0 + 512], op=ALU.mult),
                            reads=[("pb", pi), "wqk"], writes=[("wT", wi)])
                    if pend is not None:
                        pv_stage(*pend)
                    pend = (wi, kb, off, qg)
                pv_stage(*pend)
                P.dma(xb[:], xT[:, :, q0:q0 + 512].rearrange("k p t -> p k t"), writes=["xb"])
                for j in range(4):
                    pg = 6 + j % 2
                    for dc in range(16):
                        P.op("pe", lambda e, dc=dc, pg=pg, j=j: e.matmul(
                            pb[pg][:], lhsT=xb[:, dc, j * 128:(j + 1) * 128], rhs=wvg[:, dc, :], start=(dc == 0), stop=(dc == 15)),
                            reads=["wvg", "xb"], writes=[("pb", pg)])
                    P.op("act", lambda e, pg=pg: e.activation(out=sg[:], in_=pb[pg][:], func=AF.Silu),
                         reads=[("pb", pg)], writes=["sg"])
                    pj = po[j]
                    P.op("act", lambda e, pj=pj: e.activation(out=on[:], in_=pb[pj][:], func=AF.Copy), reads=[("pb", pj)], writes=["on"])
                    P.op("dve", lambda e: e.bn_stats(out=st6[:], in_=on[:]), reads=["on"], writes=["st6"])
                    P.op("dve", lambda e: e.bn_aggr(out=mv[:], in_=st6[:]), reads=["st6"], writes=["mv"])
                    P.op("act", lambda e: e.activation(out=rstd[:], in_=mv[:, 1:2], func=AF.Sqrt, bias=LN_EPS, scale=1.0),
                         reads=["mv"], writes=["rstd"])
                    P.op("dve", lambda e: e.reciprocal(out=rstd[:], in_=rstd[:]), reads=["rstd"], writes=["rstd"])
                    P.op("dve", lambda e: e.tensor_scalar(out=on[:], in0=on[:], scalar1=mv[:, 0:1], scalar2=rstd[:, 0:1],
                                                          op0=ALU.subtract, op1=ALU.mult),
                         reads=["on", "mv", "rstd"], writes=["on"])
                    P.op("dve", lambda e: e.tensor_tensor(out=on[:], in0=on[:], in1=gng[:], op=ALU.mult),
                         reads=["on", "gng"], writes=["on"])
                    ri = cnt["r"] % 2
                    cnt["r"] += 1
                    P.op("dve", lambda e, ri=ri: e.tensor_tensor(out=res[ri][:], in0=on[:], in1=sg[:], op=ALU.mult),
                         reads=["on", "sg"], writes=[("res", ri)])
                    t0 = q0 + j * 128
                    P.dma(mo[t0:t0 + 128, h * 512:(h + 1) * 512], res[ri][:], reads=[("res", ri)], queue="pool")
        P.emit()
    return nc


def build_cast(F, TW=4096):
    nc = bass.Bass("TRN2", target_bir_lowering=False)
    x = nc.dram_tensor("x", [128, F], F32, kind="ExternalInput").ap()
    y = nc.dram_tensor("y", [128, F], BF16, kind="ExternalOutput").ap()
    with contextlib.ExitStack() as st:
        NBUF = 3
        xin = [st.enter_context(nc.sbuf_tensor("xin%d" % i, [128, TW], F32)) for i in range(NBUF)]
        yo = [st.enter_context(nc.sbuf_tensor("yo%d" % i, [128, TW], BF16)) for i in range(NBUF)]
        P = Prog(nc)
        for i in range(F // TW):
            b = i % NBUF
            P.dma(xin[b][:], x[:, i * TW:(i + 1) * TW], writes=[("xin", b)])
            if i % 2 == 0:
                P.op("dve", lambda e, b=b: e.tensor_copy(out=yo[b][:], in_=xin[b][:]), reads=[("xin", b)], writes=[("yo", b)])
            else:
                P.op("act", lambda e, b=b: e.activation(out=yo[b][:], in_=xin[b][:], func=AF.Copy),
                     reads=[("xin", b)], writes=[("yo", b)])
            P.dma(y[:, i * TW:(i + 1) * TW], yo[b][:], reads=[("yo", b)], queue="pool")
        P.emit()
    return nc


N_CORES = 8
SEQ = 8192
DM = 2048
CORES = list(range(N_CORES))


def _c(a):
    return np.ascontiguousarray(a)


def _chunk_rows(w):
    return _c(w.reshape(w.shape[0] // 128, 128, w.shape[1]))


def _pdc(w):
    return _c(w.reshape(16, 128, w.shape[1]).transpose(1, 0, 2))


def _run(nc, in_maps):
    res = run_bass_kernel_spmd(nc, in_maps, core_ids=CORES)
    return res.results


def _post_inputs(mT_list, wo_b, xres_list, lnp, wq_b, k1, k2, u_b, v_b):
    import ml_dtypes
    bf = ml_dtypes.bfloat16
    KC = wo_b.shape[0] // 128
    wo_c = np.zeros((32, 128, DM), bf)
    wo_c[:KC] = wo_b.reshape(KC, 128, DM)
    wq_c = _chunk_rows(wq_b)
    k12 = _c(np.stack([k1.T, k2.T], axis=1)).astype(np.float32)
    uT_c = _c(u_b.reshape(32, 512, 16, 128).transpose(0, 3, 2, 1)).reshape(32, 128, 8192)
    vv_c = _c(v_b.reshape(32, 4, 128, DM).transpose(0, 2, 1, 3)).reshape(32, 128, 8192)
    idf = np.eye(128, dtype=np.float32)
    idb = np.eye(128).astype(bf)
    maps = []
    for j in range(N_CORES):
        mT = np.zeros((32, 128, 2048), bf)
        m = mT_list[j]
        mT[:m.shape[0]] = m
        maps.append({"mT": mT, "wo": wo_c, "xres": _c(xres_list[j]), "lnp": lnp, "wq": wq_c, "k12T": k12,
                     "identf": idf, "identb": idb, "uT": uT_c, "vv": vv_c})
    return maps


def kernel(**inp):
    import ml_dtypes
    bf = ml_dtypes.bfloat16
    f32 = np.float32
    x = np.asarray(inp["x"], f32)

    names = ["x", "l0_fox_w_in", "l0_fox_w_o", "l0_peer_wq", "l0_peer_u", "l0_peer_v",
             "l1_ret_w_in", "l1_ret_w_o", "l1_peer_wq", "l1_peer_u", "l1_peer_v"]
    sizes = [int(np.asarray(inp[n]).size) for n in names]
    total = sum(sizes)
    TW = 4096
    unit = N_CORES * 128 * TW
    padded = ((total + unit - 1) // unit) * unit
    flat = np.zeros(padded, f32)
    o = 0
    for n, s in zip(names, sizes):
        flat[o:o + s] = np.asarray(inp[n], f32).reshape(-1)
        o += s
    F = padded // (N_CORES * 128)
    flat = flat.reshape(N_CORES, 128, F)
    res = _run(build_cast(F, TW), [{"x": flat[i]} for i in range(N_CORES)])
    del flat
    fb = np.concatenate([np.asarray(r["y"]).reshape(-1) for r in res])
    cast = {}
    o = 0
    for n, s in zip(names, sizes):
        cast[n] = fb[o:o + s].reshape(np.asarray(inp[n]).shape)
        o += s
    del fb
    xb = cast["x"]

    idf = np.eye(128, dtype=f32)
    idb = np.eye(128).astype(bf)
    kk = np.arange(128)

    w0 = cast["l0_fox_w_in"]
    maskT = np.where(kk[:, None] <= kk[None, :], 0.0, NEG).astype(f32).astype(bf)
    maps = []
    for c in range(N_CORES):
        b, hg = divmod(c, 4)
        xT = _c(xb[b].T).reshape(16, 128, SEQ)
        wqk = np.stack([np.concatenate([_pdc(w0[:, h * 128:(h + 1) * 128]),
                                        _pdc(w0[:, DM + h * 128:DM + (h + 1) * 128])], axis=-1)
                        for h in range(hg * 4, hg * 4 + 4)])
        maps.append({"xT": xT, "wqk": _c(wqk), "wv": _pdc(w0[:, 2 * DM + hg * 512:2 * DM + (hg + 1) * 512]),
                     "wf": _pdc(w0[:, 3 * DM + hg * 4:3 * DM + (hg + 1) * 4]),
                     "bf": _c(np.asarray(inp["l0_fox_b_f"], f32)[hg * 4:(hg + 1) * 4].reshape(4, 1)),
                     "maskT": maskT, "identb": idb, "identf": idf})
    r1 = _run(build_fox(SEQ, 4), maps)
    oT = [np.asarray(r["oT"]) for r in r1]

    post = build_post(4096, 16)
    mT_list, xres_list = [], []
    for j in range(N_CORES):
        b, tq = divmod(j, 4)
        tsl = slice(tq * 2048, (tq + 1) * 2048)
        mT_list.append(np.concatenate([oT[b * 4 + g][:, :, tsl] for g in range(4)], axis=0))
        xres_list.append(x[b, tsl, :])
    lnp0 = _c(np.stack([inp["l0_ln1_g"], inp["l0_ln1_b"], inp["l0_ln2_g"], inp["l0_ln2_b"]]).astype(f32))
    maps = _post_inputs(mT_list, cast["l0_fox_w_o"], xres_list, lnp0, cast["l0_peer_wq"],
                        np.asarray(inp["l0_peer_k1"], f32), np.asarray(inp["l0_peer_k2"], f32),
                        cast["l0_peer_u"], cast["l0_peer_v"])
    r2 = _run(post, maps)
    x1 = np.stack([np.concatenate([np.asarray(r2[b * 4 + t]["out"]) for t in range(4)], axis=0) for b in range(2)])
    x1b = np.stack([np.concatenate([np.asarray(r2[b * 4 + t]["outb"]) for t in range(4)], axis=0) for b in range(2)])
    del maps

    w1 = cast["l1_ret_w_in"]
    inv = (10000.0 ** (-np.arange(128, dtype=f32) / 128)).astype(f32)
    ang = np.arange(SEQ, dtype=f32)[None, :] * inv[:, None]
    cosT, sinT = np.cos(ang).astype(f32), np.sin(ang).astype(f32)
    pp = np.arange(128)[:, None]
    nn = np.arange(SEQ)[None, :]
    sl_, tl_ = np.arange(128)[:, None], np.arange(128)[None, :]
    same = (sl_ // 64) == (tl_ // 64)
    gn = np.asarray(inp["l1_ret_gn_g"], f32)
    maps = []
    for c in range(N_CORES):
        b, hp = divmod(c, 4)
        heads = [2 * hp, 2 * hp + 1]
        xT = _c(x1b[b].T).reshape(16, 128, SEQ)
        wqk = np.stack([np.concatenate([_pdc(w1[:, h * 256:(h + 1) * 256]),
                                        _pdc(w1[:, DM + h * 256:DM + (h + 1) * 256])], axis=-1) for h in heads])
        wv = np.stack([_pdc(w1[:, 2 * DM + h * 512:2 * DM + (h + 1) * 512]) for h in heads])
        wg = np.stack([_pdc(w1[:, 4 * DM + h * 512:4 * DM + (h + 1) * 512]) for h in heads])
        tz = np.zeros((2, 128, SEQ), f32)
        dgm = np.zeros((2, 128, 128), f32)
        for i, h in enumerate(heads):
            lg = np.log(1.0 - 2.0 ** (-5.0 - h))
            tz[i] = np.exp(lg * np.maximum(nn - pp, 0))
            dgm[i] = np.where(same, np.exp(lg * np.abs(tl_ - sl_)),
                              np.where(sl_ // 64 < tl_ // 64, np.exp(lg * (tl_ - sl_)), 0.0))
        maps.append({"xT": xT, "wqk": _c(wqk), "wv": _c(wv), "wg": _c(wg), "cosT": cosT, "sinT": sinT,
                     "tz": tz.astype(bf), "dgm": dgm.astype(bf),
                     "gng": _c(np.stack([gn[h * 512:(h + 1) * 512] for h in heads]))})
    r3 = _run(build_ret(SEQ, 2), maps)
    mo = [np.concatenate([np.asarray(r3[b * 4 + hp]["mo"]) for hp in range(4)], axis=1) for b in range(2)]
    del maps

    mT_list, xres_list = [], []
    for j in range(N_CORES):
        b, tq = divmod(j, 4)
        tsl = slice(tq * 2048, (tq + 1) * 2048)
        mT_list.append(_c(mo[b][tsl, :].T).reshape(32, 128, 2048))
        xres_list.append(x1[b, tsl, :])
    lnp1 = _c(np.stack([inp["l1_ln1_g"], inp["l1_ln1_b"], inp["l1_ln2_g"], inp["l1_ln2_b"]]).astype(f32))
    maps = _post_inputs(mT_list, cast["l1_ret_w_o"], xres_list, lnp1, cast["l1_peer_wq"],
                        np.asarray(inp["l1_peer_k1"], f32), np.asarray(inp["l1_peer_k2"], f32),
                        cast["l1_peer_u"], cast["l1_peer_v"])
    r4 = _run(post, maps)
    out = np.stack([np.concatenate([np.asarray(r4[b * 4 + t]["out"]) for t in range(4)], axis=0) for b in range(2)])
    return out.astype(f32)
```

```python
import contextlib
import numpy as np
import concourse.bass as bass
import concourse.mybir as mybir
from concourse.bass_utils import run_bass_kernel_spmd

F32 = mybir.dt.float32
BF16 = mybir.dt.bfloat16
AF = mybir.ActivationFunctionType
ALU = mybir.AluOpType
AX = mybir.AxisListType

ENGS = ("pe", "act", "dve", "pool", "sp")
N_DMA_SEMS = 40


class _Ins:
    __slots__ = ("eng", "fn", "deps", "inc", "dma", "idx")

    def __init__(self, eng, fn, dma=None):
        self.eng = eng
        self.fn = fn
        self.deps = []
        self.inc = False
        self.dma = dma
        self.idx = None


class Prog:
    def __init__(self, nc):
        self.nc = nc
        self.streams = {e: [] for e in ENGS}
        self.last_w = {}
        self.readers = {}
        self.known = {e: {} for e in ENGS}
        self.known_dma = {e: set() for e in ENGS}
        self.dmas = []
        self.sem_uses = [0] * N_DMA_SEMS
        self.sem_last = [None] * N_DMA_SEMS

    def _need(self, ins, tok):
        if tok is None:
            return
        if tok[0] == "e":
            _, se, si = tok
            if se == ins.eng and ins.dma is None:
                return
            if self.known[ins.eng].get(se, -1) >= si:
                return
            self.known[ins.eng][se] = si
            ins.deps.append(tok)
        else:
            if tok[1] in self.known_dma[ins.eng]:
                return
            self.known_dma[ins.eng].add(tok[1])
            ins.deps.append(tok)

    def _track(self, ins, reads, writes):
        eng = ins.eng
        me = ("d", ins.dma) if ins.dma is not None else ("e", eng, ins.idx)
        for r in reads:
            w = self.last_w.get(r)
            if w is not None:
                if w[0] == "e" and w[1] == eng and ins.dma is None:
                    if eng != "pe" and self.known[eng].get(eng, -1) < w[2]:
                        self.known[eng][eng] = w[2]
                        ins.deps.append(w)
                else:
                    self._need(ins, w)
            self.readers.setdefault(r, []).append(me)
        for wr in writes:
            w = self.last_w.get(wr)
            if w is not None and not (w[0] == "e" and w[1] == eng and ins.dma is None):
                self._need(ins, w)
            for rd in self.readers.get(wr, ()):
                if rd[0] == "e" and rd[1] == eng and ins.dma is None:
                    continue
                self._need(ins, rd)
            self.readers[wr] = []
            self.last_w[wr] = me

    def op(self, eng, fn, reads=(), writes=()):
        ins = _Ins(eng, fn)
        ins.idx = len(self.streams[eng])
        self._track(ins, reads, writes)
        self.streams[eng].append(ins)
        return ins

    def dma(self, out, in_, reads=(), writes=(), queue="sp", **kw):
        did = len(self.dmas)
        slot = did % N_DMA_SEMS
        self.sem_uses[slot] += 1
        cnt = self.sem_uses[slot]
        prev = self.sem_last[slot]
        self.sem_last[slot] = did
        self.dmas.append((slot, cnt))
        ins = _Ins(queue, lambda e: e.dma_start(out=out, in_=in_, **kw), dma=did)
        ins.idx = len(self.streams[queue])
        if prev is not None:
            self._need(ins, ("d", prev))
        self._track(ins, reads, writes)
        self.streams[queue].append(ins)
        return ins

    def emit(self, final_wait_all=True):
        nc = self.nc
        for e in ENGS:
            for ins in self.streams[e]:
                for d in ins.deps:
                    if d[0] == "e":
                        self.streams[d[1]][d[2]].inc = True
        vals = {}
        for e in ENGS:
            c = 0
            for ins in self.streams[e]:
                if ins.dma is None and ins.inc:
                    c += 1
                    vals[(e, ins.idx)] = c
        with contextlib.ExitStack() as st:
            esem = {e: st.enter_context(nc.semaphore("s_" + e)) for e in ENGS}
            dsem = [st.enter_context(nc.semaphore("d_%d" % i)) for i in range(N_DMA_SEMS)]
            block = st.enter_context(nc.Block())
            dmas = self.dmas
            sem_uses = self.sem_uses

            def run(e, engine):
                for ins in self.streams[e]:
                    for d in ins.deps:
                        if d[0] == "e":
                            engine.wait_ge(esem[d[1]], vals[(d[1], d[2])])
                        else:
                            slot, cnt = dmas[d[1]]
                            engine.wait_ge(dsem[slot], 16 * cnt)
                    r = ins.fn(engine)
                    if ins.dma is not None:
                        slot, cnt = dmas[ins.dma]
                        r.then_inc(dsem[slot], 16)
                    elif ins.inc:
                        r.then_inc(esem[e], 1)
                if e == "sp" and final_wait_all:
                    for slot in range(N_DMA_SEMS):
                        if sem_uses[slot]:
                            engine.wait_ge(dsem[slot], 16 * sem_uses[slot])

            @block.sync
            def _(eng):
                run("sp", eng)

            @block.tensor
            def _(eng):
                run("pe", eng)

            @block.scalar
            def _(eng):
                run("act", eng)

            @block.vector
            def _(eng):
                run("dve", eng)

            @block.gpsimd
            def _(eng):
                run("pool", eng)


DN_ALPHA = 4 ** 0.25
LN_EPS = 1e-5
NEG = -1.0e30


def build_post(KD, NT):
    nc = bass.Bass("TRN2", target_bir_lowering=False)
    KC = KD // 128
    T = NT * 128
    mT = nc.dram_tensor("mT", [KC, 128, T], BF16, kind="ExternalInput").ap()
    wo = nc.dram_tensor("wo", [KC, 128, 2048], BF16, kind="ExternalInput").ap()
    xres = nc.dram_tensor("xres", [T, 2048], F32, kind="ExternalInput").ap()
    lnp_d = nc.dram_tensor("lnp", [4, 2048], F32, kind="ExternalInput").ap()
    wq = nc.dram_tensor("wq", [16, 128, 2048], BF16, kind="ExternalInput").ap()
    k12_d = nc.dram_tensor("k12T", [128, 2, 128], F32, kind="ExternalInput").ap()
    idf_d = nc.dram_tensor("identf", [128, 128], F32, kind="ExternalInput").ap()
    idb_d = nc.dram_tensor("identb", [128, 128], BF16, kind="ExternalInput").ap()
    uT = nc.dram_tensor("uT", [32, 128, 8192], BF16, kind="ExternalInput").ap()
    vv = nc.dram_tensor("vv", [32, 128, 8192], BF16, kind="ExternalInput").ap()
    out = nc.dram_tensor("out", [T, 2048], F32, kind="ExternalOutput").ap()
    outb = nc.dram_tensor("outb", [T, 2048], BF16, kind="ExternalOutput").ap()

    with contextlib.ExitStack() as st:
        def sb(name, shape, dt):
            return st.enter_context(nc.sbuf_tensor(name, shape, dt))

        def ps(name, shape, dt):
            return st.enter_context(nc.psum_tensor(name, shape, dt))

        lnb = sb("lnb", [128, 2048], F32)
        idf = sb("idf", [128, 128], F32)
        idb = sb("idb", [128, 128], BF16)
        k12f = sb("k12f", [128, 2, 128], F32)
        k12b = sb("k12b", [128, 2, 128], BF16)
        mt = [sb("mt%d" % i, [128, KC, 128], BF16) for i in range(1)]
        xr = sb("xr", [128, 2048], F32)
        qsb = xr
        x1 = sb("x1", [128, 2048], F32)
        x1T = sb("x1T", [128, 16, 128], BF16)
        qT = sb("qT", [128, 16, 128], BF16)
        s1 = sb("s1", [128, 8, 128], F32)
        s2 = sb("s2", [128, 8, 128], F32)
        m1 = sb("m1", [128, 8, 16], F32)
        m2 = sb("m2", [128, 8, 16], F32)
        tmp = sb("tmp", [128, 256], F32)
        cand = sb("cand", [128, 16, 16], F32)
        sc16 = sb("sc16", [128, 8, 16], F32)
        ex = sb("ex", [128, 8, 16], F32)
        zz = sb("zz", [128, 8], F32)
        nlnz = sb("nlnz", [128, 8], F32)
        st6 = sb("st6", [128, 4, 6], F32)
        mv = sb("mv", [128, 2], F32)
        rstd = sb("rstd", [128, 1], F32)
        G = sb("G", [128, 128, 128], BF16)
        GE = 16
        Db = [sb("D%d" % i, [128, GE, 128], F32) for i in range(2)]
        Eb = [sb("E%d" % i, [128, GE, 128], BF16) for i in range(2)]
        Gh = [sb("Gh%d" % i, [128, GE, 128], BF16) for i in range(2)]
        NUB = 2
        NCB = 4
        NVB = 2
        ub = [sb("ub%d" % i, [128, 16, 512], BF16) for i in range(NUB)]
        vb = [sb("vb%d" % i, [128, 4, 2048], BF16) for i in range(NVB)]
        cb = [sb("cb%d" % i, [128, 2048], BF16) for i in range(NCB)]
        gel = [sb("gel%d" % i, [128, 512], BF16) for i in range(2)]
        wsb = [sb("wsb%d" % i, [128, 512], BF16) for i in range(2)]
        wt = [sb("wt%d" % i, [128, 4, 128], BF16) for i in range(2)]

        acc4 = ps("acc4", [128, 2048], F32)
        pab = [ps("pa", [128, 512], F32), ps("pb", [128, 512], F32)]
        pg = [ps("pg0", [128, 512], BF16), ps("pg1", [128, 512], BF16)]

        P = Prog(nc)
        cnt = {"cb": 0, "ub": 0, "vb": 0, "pab": 0, "d": 0}

        P.dma(idf[:], idf_d, writes=["idf"])
        P.dma(idb[:], idb_d, writes=["idb"])
        P.dma(k12f[:], k12_d, writes=["k12f"])
        P.op("dve", lambda e: e.tensor_copy(out=k12b[:], in_=k12f[:]), reads=["k12f"], writes=["k12b"])

        def next_pab():
            i = cnt["pab"] % 2
            cnt["pab"] += 1
            return i

        def layer_norm(src_res, gi, bi, dst, dst_res):
            for q in range(4):
                P.op("dve", lambda e, q=q: e.bn_stats(out=st6[:, q, :], in_=x1[:, q * 512:(q + 1) * 512]),
                     reads=[src_res], writes=[("st6", q)])
            P.op("dve", lambda e: e.bn_aggr(out=mv[:], in_=st6[:].rearrange("p a b -> p (a b)")),
                 reads=[("st6", q) for q in range(4)], writes=["mv"])
            P.op("act", lambda e: e.activation(out=rstd[:], in_=mv[:, 1:2], func=AF.Sqrt, bias=LN_EPS, scale=1.0),
                 reads=["mv"], writes=["rstd"])
            P.op("dve", lambda e: e.reciprocal(out=rstd[:], in_=rstd[:]), reads=["rstd"], writes=["rstd"])
            P.op("dve", lambda e: e.tensor_scalar(out=x1[:], in0=x1[:], scalar1=mv[:, 0:1], scalar2=rstd[:, 0:1],
                                                  op0=ALU.subtract, op1=ALU.mult),
                 reads=[src_res, "mv", "rstd"], writes=[src_res])
            P.dma(lnb[:], lnp_d[gi:gi + 1, :].to_broadcast([128, 2048]), writes=["lnb"])
            P.op("dve", lambda e: e.tensor_tensor(out=x1[:], in0=x1[:], in1=lnb[:], op=ALU.mult),
                 reads=[src_res, "lnb"], writes=[src_res])
            P.dma(lnb[:], lnp_d[bi:bi + 1, :].to_broadcast([128, 2048]), writes=["lnb"])
            P.op("dve", lambda e: e.tensor_tensor(out=dst, in0=x1[:], in1=lnb[:], op=ALU.add),
                 reads=[src_res, "lnb"], writes=[dst_res])

        for ti in range(NT):
            tsl = slice(ti * 128, (ti + 1) * 128)
            mb = 0
            P.dma(mt[mb][:], mT[:, :, tsl].rearrange("k p t -> p k t"), writes=[("mt", mb)])
            P.dma(xr[:], xres[tsl, :], writes=["xr"])
            for kc in range(KC):
                b = cnt["cb"] % NCB
                cnt["cb"] += 1
                P.dma(cb[b][:], wo[kc], writes=[("cb", b)])
                for dq in range(4):
                    P.op("pe", lambda e, kc=kc, dq=dq, b=b, mb=mb: e.matmul(
                        acc4[:, dq * 512:(dq + 1) * 512], lhsT=mt[mb][:, kc, :], rhs=cb[b][:, dq * 512:(dq + 1) * 512],
                        start=(kc == 0), stop=(kc == KC - 1)),
                        reads=[("mt", mb), ("cb", b)], writes=["acc4"])
            P.op("dve", lambda e: e.scalar_tensor_tensor(out=x1[:], in0=xr[:], scalar=DN_ALPHA, in1=acc4[:],
                                                         op0=ALU.mult, op1=ALU.add),
                 reads=["xr", "acc4"], writes=["x1"])
            layer_norm("x1", 0, 1, x1[:], "x1")
            for g4 in range(4):
                pi = next_pab()
                for k in range(4):
                    dc = g4 * 4 + k
                    P.op("pe", lambda e, dc=dc, k=k, pi=pi: e.transpose(
                        out=pab[pi][:, k * 128:(k + 1) * 128], in_=x1[:, dc * 128:(dc + 1) * 128], identity=idf[:]),
                        reads=["x1", "idf"], writes=[("pab", pi)])
                P.op("act", lambda e, g4=g4, pi=pi: e.activation(
                    out=x1T[:, g4 * 4:(g4 + 1) * 4, :], in_=pab[pi][:].rearrange("p (a b) -> p a b", a=4), func=AF.Copy),
                    reads=[("pab", pi)], writes=["x1T"])
            for dc in range(16):
                b = cnt["cb"] % NCB
                cnt["cb"] += 1
                P.dma(cb[b][:], wq[dc], writes=[("cb", b)])
                for dq in range(4):
                    P.op("pe", lambda e, dc=dc, dq=dq, b=b: e.matmul(
                        acc4[:, dq * 512:(dq + 1) * 512], lhsT=x1T[:, dc, :], rhs=cb[b][:, dq * 512:(dq + 1) * 512],
                        start=(dc == 0), stop=(dc == 15)),
                        reads=["x1T", ("cb", b)], writes=["acc4"])
            P.op("act", lambda e: e.activation(out=qsb[:], in_=acc4[:], func=AF.Copy), reads=["acc4"], writes=["xr"])
            for g4 in range(4):
                pi = next_pab()
                for k in range(4):
                    j = g4 * 4 + k
                    P.op("pe", lambda e, j=j, k=k, pi=pi: e.transpose(
                        out=pab[pi][:, k * 128:(k + 1) * 128], in_=qsb[:, j * 128:(j + 1) * 128], identity=idf[:]),
                        reads=["xr", "idf"], writes=[("pab", pi)])
                P.op("act", lambda e, g4=g4, pi=pi: e.activation(
                    out=qT[:, g4 * 4:(g4 + 1) * 4, :], in_=pab[pi][:].rearrange("p (a b) -> p a b", a=4), func=AF.Copy),
                    reads=[("pab", pi)], writes=["qT"])
            for half, sdst, sres in ((0, s1, "s1"), (1, s2, "s2")):
                for g2 in range(2):
                    pi = next_pab()
                    for k in range(4):
                        h = g2 * 4 + k
                        P.op("pe", lambda e, h=h, k=k, pi=pi, half=half: e.matmul(
                            pab[pi][:, k * 128:(k + 1) * 128], lhsT=qT[:, 2 * h + half, :], rhs=k12b[:, half, :],
                            start=True, stop=True),
                            reads=["qT", "k12b"], writes=[("pab", pi)])
                    P.op("act", lambda e, g2=g2, pi=pi, sdst=sdst: e.activation(
                        out=sdst[:, g2 * 4:(g2 + 1) * 4, :], in_=pab[pi][:].rearrange("p (a b) -> p a b", a=4), func=AF.Copy),
                        reads=[("pab", pi)], writes=[sres])
            for sdst, sres, mm, mres in ((s1, "s1", m1, "m1"), (s2, "s2", m2, "m2")):
                for h in range(8):
                    P.op("dve", lambda e, h=h, sdst=sdst, mm=mm: e.max(out=mm[:, h, 0:8], in_=sdst[:, h, :]),
                         reads=[sres], writes=[mres])
                    P.op("dve", lambda e, h=h, sdst=sdst, mm=mm: e.match_replace(
                        out=tmp[:, 0:128], in_to_replace=mm[:, h, 0:8], in_values=sdst[:, h, :], imm_value=NEG),
                        reads=[sres, mres], writes=["tmp"])
                    P.op("dve", lambda e, h=h, mm=mm: e.max(out=mm[:, h, 8:16], in_=tmp[:, 0:128]),
                         reads=["tmp"], writes=[mres])
            for h in range(8):
                P.op("dve", lambda e, h=h: e.tensor_tensor(
                    out=cand[:], in0=m1[:, h, :].unsqueeze(2).to_broadcast([128, 16, 16]),
                    in1=m2[:, h, :].unsqueeze(1).to_broadcast([128, 16, 16]), op=ALU.add),
                    reads=["m1", "m2"], writes=["cand"])
                P.op("dve", lambda e, h=h: e.max(out=sc16[:, h, 0:8], in_=cand[:].rearrange("p a b -> p (a b)")),
                     reads=["cand"], writes=["sc16"])
                P.op("dve", lambda e, h=h: e.match_replace(
                    out=tmp[:], in_to_replace=sc16[:, h, 0:8], in_values=cand[:].rearrange("p a b -> p (a b)"),
                    imm_value=NEG), reads=["cand", "sc16"], writes=["tmp"])
                P.op("dve", lambda e, h=h: e.max(out=sc16[:, h, 8:16], in_=tmp[:]),
                     reads=["tmp"], writes=["sc16"])
                P.op("dve", lambda e, h=h: e.tensor_scalar(out=ex[:, h, :], in0=sc16[:, h, :], scalar1=sc16[:, h, 15:16],
                                                           scalar2=None, op0=ALU.subtract),
                     reads=["sc16"], writes=["ex"])
                P.op("dve", lambda e, h=h: e.tensor_scalar(out=s1[:, h, :], in0=s1[:, h, :], scalar1=sc16[:, h, 15:16],
                                                           scalar2=None, op0=ALU.subtract),
                     reads=["sc16", "s1"], writes=["s1"])
            P.op("act", lambda e: e.activation(out=ex[:], in_=ex[:], func=AF.Exp), reads=["ex"], writes=["ex"])
            P.op("dve", lambda e: e.tensor_reduce(out=zz[:], in_=ex[:], axis=AX.X, op=ALU.add),
                 reads=["ex"], writes=["zz"])
            P.op("act", lambda e: e.activation(out=zz[:], in_=zz[:], func=AF.Ln), reads=["zz"], writes=["zz"])
            P.op("dve", lambda e: e.tensor_scalar(out=nlnz[:], in0=zz[:], scalar1=-1.0, scalar2=None, op0=ALU.mult),
                 reads=["zz"], writes=["nlnz"])
            def ggen(h, gq):
                di = cnt["d"] % 2
                cnt["d"] += 1
                esl = slice(gq * GE, (gq + 1) * GE)
                P.op("dve", lambda e: e.tensor_tensor(
                    out=Db[di][:], in0=s1[:, h, esl].unsqueeze(2).to_broadcast([128, GE, 128]),
                    in1=s2[:, h, :].unsqueeze(1).to_broadcast([128, GE, 128]), op=ALU.add),
                    reads=["s1", "s2"], writes=[("D", di)])
                P.op("act", lambda e: e.activation(
                    out=Eb[di][:], in_=Db[di][:], func=AF.Exp, bias=nlnz[:, h:h + 1], scale=1.0),
                    reads=[("D", di), "nlnz"], writes=[("E", di)])
                if h == 0:
                    P.op("dve", lambda e: e.scalar_tensor_tensor(
                        out=G[:, esl, :], in0=Db[di][:], scalar=0.0, in1=Eb[di][:], op0=ALU.is_ge, op1=ALU.mult),
                        reads=[("D", di), ("E", di)], writes=[("G", gq)])
                else:
                    P.op("dve", lambda e: e.scalar_tensor_tensor(
                        out=Gh[di][:], in0=Db[di][:], scalar=0.0, in1=Eb[di][:], op0=ALU.is_ge, op1=ALU.mult),
                        reads=[("D", di), ("E", di)], writes=[("Gh", di)])
                    P.op("dve", lambda e: e.tensor_tensor(
                        out=G[:, esl, :], in0=G[:, esl, :], in1=Gh[di][:], op=ALU.add),
                        reads=[("Gh", di), ("G", gq)], writes=[("G", gq)])

            def stage_a(eg):
                u_i = cnt["ub"] % NUB
                cnt["ub"] += 1
                v_i = cnt["vb"] % NVB
                cnt["vb"] += 1
                P.dma(ub[u_i][:].rearrange("p a b -> p (a b)"), uT[eg], writes=[("ub", u_i)])
                P.dma(vb[v_i][:].rearrange("p a b -> p (a b)"), vv[eg], writes=[("vb", v_i)])
                pi = eg % 2
                for dc in range(16):
                    P.op("pe", lambda e, dc=dc: e.matmul(
                        pab[pi][:], lhsT=x1T[:, dc, :], rhs=ub[u_i][:, dc, :], start=(dc == 0), stop=(dc == 15)),
                        reads=[("ub", u_i), "x1T"], writes=[("pab", pi)])
                P.op("act", lambda e: e.activation(out=gel[pi][:], in_=pab[pi][:], func=AF.Gelu),
                     reads=[("pab", pi)], writes=[("gel", pi)])
                P.op("dve", lambda e: e.tensor_tensor(
                    out=wsb[pi][:], in0=gel[pi][:], in1=G[:, eg * 4:(eg + 1) * 4, :].rearrange("p a b -> p (a b)"), op=ALU.mult),
                    reads=[("gel", pi)] + [("G", (eg * 4 + j) // GE) for j in range(4)], writes=[("wsb", pi)])
                return (eg, pi, v_i)

            def stage_b(eg, pi, v_i):
                for j in range(4):
                    P.op("pe", lambda e, j=j: e.transpose(
                        out=pg[pi][:, j * 128:(j + 1) * 128], in_=wsb[pi][:, j * 128:(j + 1) * 128], identity=idb[:]),
                        reads=[("wsb", pi), "idb"], writes=[("pg", pi)])
                P.op("act", lambda e: e.activation(
                    out=wt[pi][:], in_=pg[pi][:].rearrange("p (a b) -> p a b", a=4), func=AF.Copy),
                    reads=[("pg", pi)], writes=[("wt", pi)])
                for j in range(4):
                    for dq in range(4):
                        P.op("pe", lambda e, j=j, dq=dq: e.matmul(
                            acc4[:, dq * 512:(dq + 1) * 512], lhsT=wt[pi][:, j, :], rhs=vb[v_i][:, j, dq * 512:(dq + 1) * 512],
                            start=(eg == 0 and j == 0), stop=(eg == 31 and j == 3)),
                            reads=[("wt", pi), ("vb", v_i)], writes=["acc4"])

            NGQ = 128 // GE
            EPG = GE // 4
            for h in range(8):
                ggen(h, 0)
            pend = None
            for gq in range(NGQ):
                for k in range(EPG):
                    eg = gq * EPG + k
                    cur = stage_a(eg)
                    if pend is not None:
                        stage_b(*pend)
                    pend = cur
                    if gq + 1 < NGQ:
                        for h in range(k * 8 // EPG, (k + 1) * 8 // EPG):
                            ggen(h, gq + 1)
            stage_b(*pend)
            P.op("dve", lambda e: e.scalar_tensor_tensor(out=x1[:], in0=x1[:], scalar=DN_ALPHA, in1=acc4[:],
                                                         op0=ALU.mult, op1=ALU.add),
                 reads=["x1", "acc4"], writes=["x1"])
            layer_norm("x1", 2, 3, xr[:], "xr")
            P.dma(out[tsl, :], xr[:], reads=["xr"], queue="pool")
            P.op("act", lambda e: e.activation(out=cb[0][:], in_=xr[:], func=AF.Copy), reads=["xr"], writes=[("cb", 0)])
            P.dma(outb[tsl, :], cb[0][:], reads=[("cb", 0)], queue="pool")
        P.emit()
    return nc


NEG = -1.0e30


def build_fox(S, NH):
    nc = bass.Bass("TRN2", target_bir_lowering=False)
    NB = S // 512
    NQ = S // 128
    NG = S // 512
    xT = nc.dram_tensor("xT", [16, 128, S], BF16, kind="ExternalInput").ap()
    wqk_d = nc.dram_tensor("wqk", [NH, 128, 16, 256], BF16, kind="ExternalInput").ap()
    wv_d = nc.dram_tensor("wv", [128, 16, NH * 128], BF16, kind="ExternalInput").ap()
    wf_d = nc.dram_tensor("wf", [128, 16, NH], BF16, kind="ExternalInput").ap()
    bf_d = nc.dram_tensor("bf", [NH, 1], F32, kind="ExternalInput").ap()
    mk_d = nc.dram_tensor("maskT", [128, 128], BF16, kind="ExternalInput").ap()
    idb_d = nc.dram_tensor("identb", [128, 128], BF16, kind="ExternalInput").ap()
    idf_d = nc.dram_tensor("identf", [128, 128], F32, kind="ExternalInput").ap()
    oT = nc.dram_tensor("oT", [NH, 128, S], BF16, kind="ExternalOutput").ap()
    negc_d = nc.dram_tensor("negc_scr", [NH, S], F32).ap()
    bsc_d = nc.dram_tensor("b_scr", [NH, 1], F32).ap()

    with contextlib.ExitStack() as st:
        def sb(name, shape, dt):
            return st.enter_context(nc.sbuf_tensor(name, shape, dt))

        def ps(name, shape, dt):
            return st.enter_context(nc.psum_tensor(name, shape, dt))

        xb = [sb("xb%d" % i, [128, 16, 512], BF16) for i in range(2)]
        w = sb("w", [128, 16, 256], BF16)
        wv = sb("wv_s", [128, 16, NH * 128], BF16)
        wf = sb("wf_s", [128, 16, NH], BF16)
        bfs = sb("bfs", [NH, 1], F32)
        nbf = sb("nbf", [NH, 1], F32)
        maskT = sb("maskT_s", [128, 128], BF16)
        idb = sb("idb", [128, 128], BF16)
        idf = sb("idf", [128, 128], F32)
        ones_b = sb("ones_b", [128, 1], BF16)
        ones_f = sb("ones_f", [1, 128], F32)
        ones_c = sb("ones_c", [128, 1], F32)
        accp = sb("accp", [128, 512], F32)
        qT = sb("qT", [128, S], BF16)
        kT = sb("kT", [128, S], BF16)
        v = sb("v", [128, NQ, NH * 128], BF16)
        lsp = sb("lsp", [NH, 512], F32)
        ngb = [sb("ngb%d" % i, [NH, 512], F32) for i in range(2)]
        sq = sb("sq", [128, 512], BF16)
        qmx = sb("qmx", [1, NB], F32)
        kmx = sb("kmx", [1, NB], F32)
        q2 = sb("q2", [1, 1], F32)
        k2 = sb("k2", [1, 1], F32)
        Bb = sb("Bb", [128, 1], F32)
        refb = sb("refb", [128, NG], F32)
        ncr = sb("ncr", [NQ, 128], F32)
        negc_col = sb("negc_col", [128, NQ], F32)
        biasmat = sb("biasmat", [128, NG, NQ], F32)
        pT = [sb("pT%d" % i, [128, 512], BF16) for i in range(2)]
        rrow = sb("rrow", [1, 512], F32)
        osb = sb("osb", [128, 512], F32)
        oTs = sb("oTs", [128, S], BF16)

        pq = [ps("pq%d" % i, [128, 512], F32) for i in range(4)]
        po = ps("po", [128, 512], F32)
        psm = ps("psm", [128, 512], F32)
        pbc = ps("pbc", [128, 512], F32)
        pn = ps("pn", [128, 512], F32)

        P = Prog(nc)
        cnt = {"pq": 0, "x": 0, "pt": 0}

        def nb():
            i = cnt["pq"] % 4
            cnt["pq"] += 1
            return i

        P.dma(wv[:], wv_d, writes=["wv"])
        P.dma(wf[:], wf_d, writes=["wf"])
        P.dma(bfs[:], bf_d, writes=["bfs"])
        P.dma(maskT[:], mk_d, writes=["maskT"])
        P.dma(idb[:], idb_d, writes=["idb"])
        P.dma(idf[:], idf_d, writes=["idf"])
        P.op("dve", lambda e: e.tensor_scalar(out=nbf[:], in0=bfs[:], scalar1=-1.0, scalar2=None, op0=ALU.mult),
             reads=["bfs"], writes=["nbf"])
        P.op("pool", lambda e: e.memset(ones_b[:], 1.0), writes=["ones_b"])
        P.op("pool", lambda e: e.memset(ones_f[:], 1.0), writes=["ones_f"])
        P.op("pool", lambda e: e.memset(ones_c[:], 1.0), writes=["ones_c"])

        def load_x(blk):
            xi = cnt["x"] % 2
            cnt["x"] += 1
            P.dma(xb[xi][:], xT[:, :, blk * 512:(blk + 1) * 512].rearrange("k p t -> p k t"), writes=[("xb", xi)])
            return xi

        for blk in range(NB):
            xi = load_x(blk)
            bsl = slice(blk * 512, (blk + 1) * 512)
            for tt in range(4):
                pi = nb()
                for dc in range(16):
                    P.op("pe", lambda e, dc=dc, pi=pi, xi=xi, tt=tt: e.matmul(
                        pq[pi][:, 0:NH * 128], lhsT=xb[xi][:, dc, tt * 128:(tt + 1) * 128], rhs=wv[:, dc, :],
                        start=(dc == 0), stop=(dc == 15)),
                        reads=["wv", ("xb", xi)], writes=[("pq", pi)])
                eng = "act" if tt % 2 == 0 else "dve"
                if eng == "act":
                    P.op("act", lambda e, pi=pi, blk=blk, tt=tt: e.activation(
                        out=v[:, blk * 4 + tt, :], in_=pq[pi][:, 0:NH * 128], func=AF.Copy),
                        reads=[("pq", pi)], writes=["v"])
                else:
                    P.op("dve", lambda e, pi=pi, blk=blk, tt=tt: e.tensor_copy(
                        out=v[:, blk * 4 + tt, :], in_=pq[pi][:, 0:NH * 128]),
                        reads=[("pq", pi)], writes=["v"])
            for dc in range(16):
                P.op("pe", lambda e, dc=dc, xi=xi: e.matmul(
                    pn[0:NH, :], lhsT=wf[:, dc, :], rhs=xb[xi][:, dc, :], start=(dc == 0), stop=(dc == 15)),
                    reads=["wf", ("xb", xi)], writes=["pn"])
            P.op("act", lambda e: e.activation(out=lsp[:], in_=pn[0:NH, :], func=AF.Exp, bias=nbf[:, 0:1], scale=-1.0),
                 reads=["pn", "nbf"], writes=["lsp"])
            P.op("act", lambda e: e.activation(out=lsp[:], in_=lsp[:], func=AF.Ln, bias=1.0, scale=1.0),
                 reads=["lsp"], writes=["lsp"])
            gi = blk % 2
            if blk == 0:
                P.op("dve", lambda e, gi=gi: e.tensor_tensor_scan(
                    out=ngb[gi][:], data0=lsp[:], data1=lsp[:], initial=0.0, op0=ALU.add, op1=ALU.max),
                    reads=["lsp"], writes=[("ngb", gi)])
            else:
                P.op("dve", lambda e, gi=gi: e.tensor_tensor_scan(
                    out=ngb[gi][:], data0=lsp[:], data1=lsp[:], initial=ngb[1 - gi][:, 511:512], op0=ALU.add, op1=ALU.max),
                    reads=["lsp", ("ngb", 1 - gi)], writes=[("ngb", gi)])
            P.dma(negc_d[:, bsl], ngb[gi][:], reads=[("ngb", gi)], writes=["negc_d"], queue="pool")

        for h in range(NH):
            P.dma(w[:], wqk_d[h], writes=["w"])
            for blk in range(NB):
                xi = load_x(blk)
                bsl = slice(blk * 512, (blk + 1) * 512)
                for which, dst, dres, scale, mxt, mres in ((0, qT, "qT", 128 ** -0.5, qmx, "qmx"), (1, kT, "kT", 1.0, kmx, "kmx")):
                    pi = nb()
                    for dc in range(16):
                        P.op("pe", lambda e, dc=dc, pi=pi, xi=xi, which=which: e.matmul(
                            pq[pi][:], lhsT=w[:, dc, which * 128:(which + 1) * 128], rhs=xb[xi][:, dc, :],
                            start=(dc == 0), stop=(dc == 15)),
                            reads=["w", ("xb", xi)], writes=[("pq", pi)])
                    P.op("act", lambda e, pi=pi, dst=dst, bsl=bsl, scale=scale: e.activation(
                        out=dst[:, bsl], in_=pq[pi][:], func=AF.Copy, scale=scale),
                        reads=[("pq", pi)], writes=[dres])
                    P.op("dve", lambda e, dst=dst, bsl=bsl: e.tensor_tensor(out=sq[:], in0=dst[:, bsl], in1=dst[:, bsl], op=ALU.mult),
                         reads=[dres], writes=["sq"])
                    P.op("pe", lambda e: e.matmul(pn[0:1, :], lhsT=ones_b[:], rhs=sq[:], start=True, stop=True),
                         reads=["ones_b", "sq"], writes=["pn"])
                    P.op("dve", lambda e, mxt=mxt, blk=blk: e.tensor_reduce(
                        out=mxt[:, blk:blk + 1], in_=pn[0:1, :], axis=AX.X, op=ALU.max),
                        reads=["pn"], writes=[mres])
            P.op("dve", lambda e: e.tensor_reduce(out=q2[:], in_=qmx[:], axis=AX.X, op=ALU.max), reads=["qmx"], writes=["q2"])
            P.op("dve", lambda e: e.tensor_reduce(out=k2[:], in_=kmx[:], axis=AX.X, op=ALU.max), reads=["kmx"], writes=["k2"])
            P.op("dve", lambda e: e.tensor_tensor(out=q2[:], in0=q2[:], in1=k2[:], op=ALU.mult), reads=["q2", "k2"], writes=["q2"])
            P.op("act", lambda e: e.activation(out=q2[:], in_=q2[:], func=AF.Sqrt, scale=1.1025), reads=["q2"], writes=["q2"])
            P.dma(bsc_d[h:h + 1, :], q2[:], reads=["q2"], writes=[("bsc", h)], queue="pool")
            P.dma(Bb[:], bsc_d[h:h + 1, :].to_broadcast([128, 1]), reads=[("bsc", h)], writes=["Bb"])
            P.dma(refb[:], negc_d[h:h + 1, :].rearrange("o (g c) -> o g c", c=512)[:, :, 511].to_broadcast([128, NG]),
                  reads=["negc_d"], writes=["refb"], allow_slow_non_contiguous=True)
            P.op("dve", lambda e: e.tensor_scalar(out=refb[:], in0=refb[:], scalar1=Bb[:, 0:1], scalar2=None, op0=ALU.add),
                 reads=["refb", "Bb"], writes=["refb"])
            P.dma(ncr[:], negc_d[h:h + 1, :].rearrange("o (k p) -> (o k) p", p=128), reads=["negc_d"], writes=["ncr"])
            P.op("pe", lambda e: e.transpose(out=pn[:, 0:NQ], in_=ncr[:], identity=idf[0:NQ, 0:NQ]),
                 reads=["ncr", "idf"], writes=["pn"])
            P.op("dve", lambda e: e.tensor_copy(out=negc_col[:], in_=pn[:, 0:NQ]), reads=["pn"], writes=["negc_col"])
            for qg in range(NG):
                P.op("dve", lambda e, qg=qg: e.tensor_scalar(
                    out=biasmat[:, qg, :], in0=negc_col[:], scalar1=refb[:, qg:qg + 1], scalar2=None, op0=ALU.subtract),
                    reads=["negc_col", "refb"], writes=["biasmat"])
            def pv_stage(ti, kb, off, N, nkb, h):
                P.op("pe", lambda e: e.matmul(
                    po[:, off:512], lhsT=v[:, kb, h * 128:(h + 1) * 128], rhs=pT[ti][:, 0:N],
                    start=(kb == 0), stop=(kb == nkb - 1)),
                    reads=["v", ("pT", ti)], writes=["po"])
                if kb == 0:
                    P.op("dve", lambda e: e.tensor_copy(out=accp[:], in_=pT[ti][:]), reads=[("pT", ti)], writes=["accp"])
                else:
                    P.op("dve", lambda e: e.tensor_tensor(out=accp[:, off:512], in0=accp[:, off:512], in1=pT[ti][:, 0:N], op=ALU.add),
                         reads=[("pT", ti), "accp"], writes=["accp"])

            for qg in range(NG):
                q0 = qg * 512
                nkb = 4 * qg + 4
                pend = None
                for kb in range(nkb):
                    off = max(0, kb * 128 - q0)
                    N = 512 - off
                    diag = kb >= 4 * qg
                    pi = nb()
                    ti = cnt["pt"] % 2
                    cnt["pt"] += 1
                    P.op("pe", lambda e, pi=pi, kb=kb, q0=q0, off=off, N=N, diag=diag: e.matmul(
                        pq[pi][:, 0:N], lhsT=kT[:, kb * 128:(kb + 1) * 128], rhs=qT[:, q0 + off:q0 + 512],
                        start=True, stop=(not diag)),
                        reads=["qT", "kT"], writes=[("pq", pi)])
                    if diag:
                        P.op("pe", lambda e, pi=pi: e.matmul(
                            pq[pi][:, 0:128], lhsT=idb[:], rhs=maskT[:], start=False, stop=True),
                            reads=["idb", "maskT"], writes=[("pq", pi)])
                    P.op("act", lambda e, pi=pi, ti=ti, N=N, qg=qg, kb=kb: e.activation(
                        out=pT[ti][:, 0:N], in_=pq[pi][:, 0:N], func=AF.Exp, bias=biasmat[:, qg, kb:kb + 1], scale=1.0),
                        reads=[("pq", pi), "biasmat"], writes=[("pT", ti)])
                    if pend is not None:
                        pv_stage(*pend)
                    pend = (ti, kb, off, N, nkb, h)
                pv_stage(*pend)
                P.op("pe", lambda e: e.matmul(psm[0:1, :], lhsT=ones_c[:], rhs=accp[:], start=True, stop=True),
                     reads=["ones_c", "accp"], writes=["psm"])
                P.op("dve", lambda e: e.reciprocal(out=rrow[:], in_=psm[0:1, :]), reads=["psm"], writes=["rrow"])
                P.op("pe", lambda e: e.matmul(pbc[:], lhsT=ones_f[:], rhs=rrow[:], start=True, stop=True),
                     reads=["ones_f", "rrow"], writes=["pbc"])
                P.op("act", lambda e: e.activation(out=osb[:], in_=po[:], func=AF.Copy), reads=["po"], writes=["osb"])
                P.op("dve", lambda e, q0=q0: e.tensor_tensor(out=oTs[:, q0:q0 + 512], in0=osb[:], in1=pbc[:], op=ALU.mult),
                     reads=["osb", "pbc"], writes=["oTs"])
            P.dma(oT[h], oTs[:], reads=["oTs"], queue="pool")
        P.emit()
    return nc


LN_EPS = 1e-5


def build_ret(S, NH):
    nc = bass.Bass("TRN2", target_bir_lowering=False)
    NB = S // 512
    NQ = S // 128
    NG = S // 512
    xT = nc.dram_tensor("xT", [16, 128, S], BF16, kind="ExternalInput").ap()
    wqk_d = nc.dram_tensor("wqk", [NH, 128, 16, 512], BF16, kind="ExternalInput").ap()
    wv_d = nc.dram_tensor("wv", [NH, 128, 16, 512], BF16, kind="ExternalInput").ap()
    wg_d = nc.dram_tensor("wg", [NH, 128, 16, 512], BF16, kind="ExternalInput").ap()
    cos_d = nc.dram_tensor("cosT", [128, S], F32, kind="ExternalInput").ap()
    sin_d = nc.dram_tensor("sinT", [128, S], F32, kind="ExternalInput").ap()
    tz_d = nc.dram_tensor("tz", [NH, 128, S], BF16, kind="ExternalInput").ap()
    dg_d = nc.dram_tensor("dgm", [NH, 128, 128], BF16, kind="ExternalInput").ap()
    gn_d = nc.dram_tensor("gng", [NH, 512], F32, kind="ExternalInput").ap()
    mo = nc.dram_tensor("mo", [S, NH * 512], BF16, kind="ExternalOutput").ap()

    with contextlib.ExitStack() as st:
        def sb(name, shape, dt):
            return st.enter_context(nc.sbuf_tensor(name, shape, dt))

        def ps(name, shape, dt):
            return st.enter_context(nc.psum_tensor(name, shape, dt))

        xb = sb("xb", [128, 16, 512], BF16)
        wqk = sb("wqk_s", [128, 16, 512], BF16)
        assert S == 8192 or S <= 8192
        tzv = wqk[:].rearrange("p a b -> p (a b)")
        wvg = sb("wvg_s", [128, 16, 512], BF16)
        cs = sb("cs", [128, 512], F32)
        sn = sb("sn", [128, 512], F32)
        tt = [sb("tt%d" % i, [128, 512], F32) for i in range(2)]
        qT = [sb("qT%d" % i, [128, S], BF16) for i in range(2)]
        kT = [sb("kT%d" % i, [128, S], BF16) for i in range(2)]
        v = sb("v", [128, NQ, 512], BF16)
        dgm = sb("dgm_s", [128, 128], BF16)
        gng = sb("gng_s", [128, 512], F32)
        wT = [sb("wT%d" % i, [128, 512], BF16) for i in range(2)]
        st6 = sb("st6", [128, 6], F32)
        mv = sb("mv", [128, 2], F32)
        rstd = sb("rstd", [128, 1], F32)
        on = sb("on", [128, 512], F32)
        sg = sb("sg", [128, 512], F32)
        tt = tt + [on, sg]
        ttres = [("tt", 0), ("tt", 1), "on", "sg"]
        res = [sb("res%d" % i, [128, 512], BF16) for i in range(2)]

        pb = [ps("pb%d" % i, [128, 512], F32) for i in range(8)]

        P = Prog(nc)
        cnt = {"p": 0, "w": 0, "r": 0}

        def nb():
            i = cnt["p"] % 8
            cnt["p"] += 1
            return i

        for h in range(NH):
            P.dma(wqk[:], wqk_d[h], writes=["wqk"])
            P.dma(wvg[:], wv_d[h], writes=["wvg"])
            P.dma(dgm[:], dg_d[h], writes=["dgm"])
            P.dma(gng[:], gn_d[h:h + 1, :].to_broadcast([128, 512]), writes=["gng"])
            for blk in range(NB):
                bsl = slice(blk * 512, (blk + 1) * 512)
                P.dma(xb[:], xT[:, :, bsl].rearrange("k p t -> p k t"), writes=["xb"])
                P.dma(cs[:], cos_d[:, bsl], writes=["cs"])
                P.dma(sn[:], sin_d[:, bsl], writes=["sn"])
                for which, dst, dres, scale in ((0, qT, "qT", 1.0), (1, kT, "kT", 0.0625)):
                    pa = nb()
                    pc = nb()
                    for half, pi in ((0, pa), (1, pc)):
                        c0 = which * 256 + half * 128
                        for dc in range(16):
                            P.op("pe", lambda e, dc=dc, pi=pi, c0=c0: e.matmul(
                                pb[pi][:], lhsT=wqk[:, dc, c0:c0 + 128], rhs=xb[:, dc, :], start=(dc == 0), stop=(dc == 15)),
                                reads=["wqk", "xb"], writes=[("pb", pi)])
                    for ti, (src, tab, tres) in enumerate(((pa, cs, "cs"), (pc, sn, "sn"), (pa, sn, "sn"), (pc, cs, "cs"))):
                        P.op("dve", lambda e, ti=ti, src=src, tab=tab, scale=scale: e.scalar_tensor_tensor(
                            out=tt[ti][:], in0=pb[src][:], scalar=scale, in1=tab[:], op0=ALU.mult, op1=ALU.mult),
                            reads=[("pb", src), tres], writes=[ttres[ti]])
                    P.op("dve", lambda e, dst=dst, bsl=bsl: e.tensor_tensor(out=dst[0][:, bsl], in0=tt[0][:], in1=tt[1][:], op=ALU.subtract),
                         reads=[ttres[0], ttres[1]], writes=[dres + "0"])
                    P.op("dve", lambda e, dst=dst, bsl=bsl: e.tensor_tensor(out=dst[1][:, bsl], in0=tt[2][:], in1=tt[3][:], op=ALU.add),
                         reads=[ttres[2], ttres[3]], writes=[dres + "1"])
                for t4 in range(4):
                    pi = nb()
                    for dc in range(16):
                        P.op("pe", lambda e, dc=dc, pi=pi, t4=t4: e.matmul(
                            pb[pi][:], lhsT=xb[:, dc, t4 * 128:(t4 + 1) * 128], rhs=wvg[:, dc, :], start=(dc == 0), stop=(dc == 15)),
                            reads=["wvg", "xb"], writes=[("pb", pi)])
                    P.op("act", lambda e, pi=pi, blk=blk, t4=t4: e.activation(out=v[:, blk * 4 + t4, :], in_=pb[pi][:], func=AF.Copy),
                         reads=[("pb", pi)], writes=["v"])
            P.dma(tzv[:, 0:S], tz_d[h], writes=["wqk"])
            P.dma(wvg[:], wg_d[h], writes=["wvg"])
            po = [0, 1, 2, 3]

            def pv_stage(wi, kb, off, qg):
                for j in range(off // 128, 4):
                    tb = 4 * qg + j
                    c0 = j * 128 - off
                    P.op("pe", lambda e, j=j, c0=c0, tb=tb: e.matmul(
                        pb[po[j]][:], lhsT=wT[wi][:, c0:c0 + 128], rhs=v[:, kb, :], start=(kb == 0), stop=(kb == tb)),
                        reads=[("wT", wi), "v"], writes=[("pb", po[j])])

            for qg in range(NG):
                q0 = qg * 512
                nkb = 4 * qg + 4
                pend = None
                for kb in range(nkb):
                    off = max(0, kb * 128 - q0)
                    N = 512 - off
                    diag = kb >= 4 * qg
                    pi = 4 + cnt["w"] % 2
                    wi = cnt["w"] % 2
                    cnt["w"] += 1
                    for c in range(2):
                        P.op("pe", lambda e, pi=pi, kb=kb, q0=q0, off=off, N=N, c=c: e.matmul(
                            pb[pi][:, 0:N], lhsT=kT[c][:, kb * 128:(kb + 1) * 128], rhs=qT[c][:, q0 + off:q0 + 512],
                            start=(c == 0), stop=(c == 1)),
                            reads=["qT0", "qT1", "kT0", "kT1"], writes=[("pb", pi)])
                    if diag:
                        P.op("dve", lambda e, pi=pi, wi=wi: e.tensor_tensor(
                            out=wT[wi][:, 0:128], in0=pb[pi][:, 0:128], in1=dgm[:], op=ALU.mult),
                            reads=[("pb", pi), "dgm"], writes=[("wT", wi)])
                        if N > 128:
                            P.op("dve", lambda e, pi=pi, wi=wi, N=N: e.tensor_tensor(
                                out=wT[wi][:, 128:N], in0=pb[pi][:, 128:N], in1=tzv[:, 128:N], op=ALU.mult),
                                reads=[("pb", pi), "wqk"], writes=[("wT", wi)])
                    else:
                        n0 = q0 - kb * 128
                        P.op("dve", lambda e, pi=pi, wi=wi, n0=n0: e.tensor_tensor(
                            out=wT[wi][:], in0=pb[pi][:], in1=tzv[:, n0:n0 + 512], op=ALU.mult),
                            reads=[("pb", pi), "wqk"], writes=[("wT", wi)])
                    if pend is not None:
                        pv_stage(*pend)
                    pend = (wi, kb, off, qg)
                pv_stage(*pend)
                P.dma(xb[:], xT[:, :, q0:q0 + 512].rearrange("k p t -> p k t"), writes=["xb"])
                for j in range(4):
                    pg = 6 + j % 2
                    for dc in range(16):
                        P.op("pe", lambda e, dc=dc, pg=pg, j=j: e.matmul(
                            pb[pg][:], lhsT=xb[:, dc, j * 128:(j + 1) * 128], rhs=wvg[:, dc, :], start=(dc == 0), stop=(dc == 15)),
                            reads=["wvg", "xb"], writes=[("pb", pg)])
                    P.op("act", lambda e, pg=pg: e.activation(out=sg[:], in_=pb[pg][:], func=AF.Silu),
                         reads=[("pb", pg)], writes=["sg"])
                    pj = po[j]
                    P.op("act", lambda e, pj=pj: e.activation(out=on[:], in_=pb[pj][:], func=AF.Copy), reads=[("pb", pj)], writes=["on"])
                    P.op("dve", lambda e: e.bn_stats(out=st6[:], in_=on[:]), reads=["on"], writes=["st6"])
                    P.op("dve", lambda e: e.bn_aggr(out=mv[:], in_=st6[:]), reads=["st6"], writes=["mv"])
                    P.op("act", lambda e: e.activation(out=rstd[:], in_=mv[:, 1:2], func=AF.Sqrt, bias=LN_EPS, scale=1.0),
                         reads=["mv"], writes=["rstd"])
                    P.op("dve", lambda e: e.reciprocal(out=rstd[:], in_=rstd[:]), reads=["rstd"], writes=["rstd"])
                    P.op("dve", lambda e: e.tensor_scalar(out=on[:], in0=on[:], scalar1=mv[:, 0:1], scalar2=rstd[:, 0:1],
                                                          op0=ALU.subtract, op1=ALU.mult),
                         reads=["on", "mv", "rstd"], writes=["on"])
                    P.op("dve", lambda e: e.tensor_tensor(out=on[:], in0=on[:], in1=gng[:], op=ALU.mult),
                         reads=["on", "gng"], writes=["on"])
                    ri = cnt["r"] % 2
                    cnt["r"] += 1
                    P.op("dve", lambda e, ri=ri: e.tensor_tensor(out=res[ri][:], in0=on[:], in1=sg[:], op=ALU.mult),
                         reads=["on", "sg"], writes=[("res", ri)])
                    t0 = q0 + j * 128
                    P.dma(mo[t0:t0 + 128, h * 512:(h + 1) * 512], res[ri][:], reads=[("res", ri)], queue="pool")
        P.emit()
    return nc


def build_cast(F, TW=4096):
    nc = bass.Bass("TRN2", target_bir_lowering=False)
    x = nc.dram_tensor("x", [128, F], F32, kind="ExternalInput").ap()
    y = nc.dram_tensor("y", [128, F], BF16, kind="ExternalOutput").ap()
    with contextlib.ExitStack() as st:
        NBUF = 3
        xin = [st.enter_context(nc.sbuf_tensor("xin%d" % i, [128, TW], F32)) for i in range(NBUF)]
        yo = [st.enter_context(nc.sbuf_tensor("yo%d" % i, [128, TW], BF16)) for i in range(NBUF)]
        P = Prog(nc)
        for i in range(F // TW):
            b = i % NBUF
            P.dma(xin[b][:], x[:, i * TW:(i + 1) * TW], writes=[("xin", b)])
            if i % 2 == 0:
                P.op("dve", lambda e, b=b: e.tensor_copy(out=yo[b][:], in_=xin[b][:]), reads=[("xin", b)], writes=[("yo", b)])
            else:
                P.op("act", lambda e, b=b: e.activation(out=yo[b][:], in_=xin[b][:], func=AF.Copy),
                     reads=[("xin", b)], writes=[("yo", b)])
            P.dma(y[:, i * TW:(i + 1) * TW], yo[b][:], reads=[("yo", b)], queue="pool")
        P.emit()
    return nc


N_CORES = 8
SEQ = 8192
DM = 2048
CORES = list(range(N_CORES))


def _c(a):
    return np.ascontiguousarray(a)


def _chunk_rows(w):
    return _c(w.reshape(w.shape[0] // 128, 128, w.shape[1]))


def _pdc(w):
    return _c(w.reshape(16, 128, w.shape[1]).transpose(1, 0, 2))


def _run(nc, in_maps):
    res = run_bass_kernel_spmd(nc, in_maps, core_ids=CORES)
    return res.results


def _post_inputs(mT_list, wo_b, xres_list, lnp, wq_b, k1, k2, u_b, v_b, KCP=32):
    import ml_dtypes
    bf = ml_dtypes.bfloat16
    KC = wo_b.shape[0] // 128
    wo_c = np.zeros((KCP, 128, DM), bf)
    wo_c[:KC] = wo_b.reshape(KC, 128, DM)
    wq_c = _chunk_rows(wq_b)
    k12 = _c(np.stack([k1.T, k2.T], axis=1)).astype(np.float32)
    uT_c = _c(u_b.reshape(32, 512, 16, 128).transpose(0, 3, 2, 1)).reshape(32, 128, 8192)
    vv_c = _c(v_b.reshape(32, 4, 128, DM).transpose(0, 2, 1, 3)).reshape(32, 128, 8192)
    idf = np.eye(128, dtype=np.float32)
    idb = np.eye(128).astype(bf)
    maps = []
    for j in range(N_CORES):
        mT = np.zeros((KCP, 128, 2048), bf)
        m = mT_list[j]
        mT[:m.shape[0]] = m
        maps.append({"mT": mT, "wo": wo_c, "xres": _c(xres_list[j]), "lnp": lnp, "wq": wq_c, "k12T": k12,
                     "identf": idf, "identb": idb, "uT": uT_c, "vv": vv_c})
    return maps


def kernel(**inp):
    import ml_dtypes
    bf = ml_dtypes.bfloat16
    f32 = np.float32
    x = np.asarray(inp["x"], f32)

    names = ["x", "l0_fox_w_in", "l0_fox_w_o", "l0_peer_wq", "l0_peer_u", "l0_peer_v",
             "l1_ret_w_in", "l1_ret_w_o", "l1_peer_wq", "l1_peer_u", "l1_peer_v"]
    sizes = [int(np.asarray(inp[n]).size) for n in names]
    total = sum(sizes)
    TW = 4096
    unit = N_CORES * 128 * TW
    padded = ((total + unit - 1) // unit) * unit
    flat = np.zeros(padded, f32)
    o = 0
    for n, s in zip(names, sizes):
        flat[o:o + s] = np.asarray(inp[n], f32).reshape(-1)
        o += s
    F = padded // (N_CORES * 128)
    flat = flat.reshape(N_CORES, 128, F)
    res = _run(build_cast(F, TW), [{"x": flat[i]} for i in range(N_CORES)])
    del flat
    fb = np.concatenate([np.asarray(r["y"]).reshape(-1) for r in res])
    cast = {}
    o = 0
    for n, s in zip(names, sizes):
        cast[n] = fb[o:o + s].reshape(np.asarray(inp[n]).shape)
        o += s
    del fb
    xb = cast["x"]

    idf = np.eye(128, dtype=f32)
    idb = np.eye(128).astype(bf)
    kk = np.arange(128)

    w0 = cast["l0_fox_w_in"]
    maskT = np.where(kk[:, None] <= kk[None, :], 0.0, NEG).astype(f32).astype(bf)
    maps = []
    for c in range(N_CORES):
        b, hg = divmod(c, 4)
        xT = _c(xb[b].T).reshape(16, 128, SEQ)
        wqk = np.stack([np.concatenate([_pdc(w0[:, h * 128:(h + 1) * 128]),
                                        _pdc(w0[:, DM + h * 128:DM + (h + 1) * 128])], axis=-1)
                        for h in range(hg * 4, hg * 4 + 4)])
        maps.append({"xT": xT, "wqk": _c(wqk), "wv": _pdc(w0[:, 2 * DM + hg * 512:2 * DM + (hg + 1) * 512]),
                     "wf": _pdc(w0[:, 3 * DM + hg * 4:3 * DM + (hg + 1) * 4]),
                     "bf": _c(np.asarray(inp["l0_fox_b_f"], f32)[hg * 4:(hg + 1) * 4].reshape(4, 1)),
                     "maskT": maskT, "identb": idb, "identf": idf})
    r1 = _run(build_fox(SEQ, 4), maps)
    oT = [np.asarray(r["oT"]) for r in r1]

    post0 = build_post(2048, 16)
    mT_list, xres_list = [], []
    for j in range(N_CORES):
        b, tq = divmod(j, 4)
        tsl = slice(tq * 2048, (tq + 1) * 2048)
        mT_list.append(np.concatenate([oT[b * 4 + g][:, :, tsl] for g in range(4)], axis=0))
        xres_list.append(x[b, tsl, :])
    lnp0 = _c(np.stack([inp["l0_ln1_g"], inp["l0_ln1_b"], inp["l0_ln2_g"], inp["l0_ln2_b"]]).astype(f32))
    maps = _post_inputs(mT_list, cast["l0_fox_w_o"], xres_list, lnp0, cast["l0_peer_wq"],
                        np.asarray(inp["l0_peer_k1"], f32), np.asarray(inp["l0_peer_k2"], f32),
                        cast["l0_peer_u"], cast["l0_peer_v"], KCP=16)
    r2 = _run(post0, maps)
    x1 = np.stack([np.concatenate([np.asarray(r2[b * 4 + t]["out"]) for t in range(4)], axis=0) for b in range(2)])
    x1b = np.stack([np.concatenate([np.asarray(r2[b * 4 + t]["outb"]) for t in range(4)], axis=0) for b in range(2)])
    del maps

    w1 = cast["l1_ret_w_in"]
    inv = (10000.0 ** (-np.arange(128, dtype=f32) / 128)).astype(f32)
    ang = np.arange(SEQ, dtype=f32)[None, :] * inv[:, None]
    cosT, sinT = np.cos(ang).astype(f32), np.sin(ang).astype(f32)
    pp = np.arange(128)[:, None]
    nn = np.arange(SEQ)[None, :]
    sl_, tl_ = np.arange(128)[:, None], np.arange(128)[None, :]
    same = (sl_ // 64) == (tl_ // 64)
    gn = np.asarray(inp["l1_ret_gn_g"], f32)
    maps = []
    for c in range(N_CORES):
        b, hp = divmod(c, 4)
        heads = [2 * hp, 2 * hp + 1]
        xT = _c(x1b[b].T).reshape(16, 128, SEQ)
        wqk = np.stack([np.concatenate([_pdc(w1[:, h * 256:(h + 1) * 256]),
                                        _pdc(w1[:, DM + h * 256:DM + (h + 1) * 256])], axis=-1) for h in heads])
        wv = np.stack([_pdc(w1[:, 2 * DM + h * 512:2 * DM + (h + 1) * 512]) for h in heads])
        wg = np.stack([_pdc(w1[:, 4 * DM + h * 512:4 * DM + (h + 1) * 512]) for h in heads])
        tz = np.zeros((2, 128, SEQ), f32)
        dgm = np.zeros((2, 128, 128), f32)
        for i, h in enumerate(heads):
            lg = np.log(1.0 - 2.0 ** (-5.0 - h))
            tz[i] = np.exp(lg * np.maximum(nn - pp, 0))
            dgm[i] = np.where(same, np.exp(lg * np.abs(tl_ - sl_)),
                              np.where(sl_ // 64 < tl_ // 64, np.exp(lg * (tl_ - sl_)), 0.0))
        maps.append({"xT": xT, "wqk": _c(wqk), "wv": _c(wv), "wg": _c(wg), "cosT": cosT, "sinT": sinT,
                     "tz": tz.astype(bf), "dgm": dgm.astype(bf),
                     "gng": _c(np.stack([gn[h * 512:(h + 1) * 512] for h in heads]))})
    r3 = _run(build_ret(SEQ, 2), maps)
    mo = [np.concatenate([np.asarray(r3[b * 4 + hp]["mo"]) for hp in range(4)], axis=1) for b in range(2)]
    del maps

    mT_list, xres_list = [], []
    for j in range(N_CORES):
        b, tq = divmod(j, 4)
        tsl = slice(tq * 2048, (tq + 1) * 2048)
        mT_list.append(_c(mo[b][tsl, :].T).reshape(32, 128, 2048))
        xres_list.append(x1[b, tsl, :])
    lnp1 = _c(np.stack([inp["l1_ln1_g"], inp["l1_ln1_b"], inp["l1_ln2_g"], inp["l1_ln2_b"]]).astype(f32))
    maps = _post_inputs(mT_list, cast["l1_ret_w_o"], xres_list, lnp1, cast["l1_peer_wq"],
                        np.asarray(inp["l1_peer_k1"], f32), np.asarray(inp["l1_peer_k2"], f32),
                        cast["l1_peer_u"], cast["l1_peer_v"])
    r4 = _run(build_post(4096, 16), maps)
    out = np.stack([np.concatenate([np.asarray(r4[b * 4 + t]["out"]) for t in range(4)], axis=0) for b in range(2)])
    return out.astype(f32)
```
